# Optimizing a Trainium2 kernel written in Bass

```python
import functools
import jax, jax.numpy as jnp
from jax import lax
import numpy as np


D_MODEL = 2048
BATCH = 16
SEQ = 2048
DEPTH = 1
DEC_BATCH = 8
DEC_SEQ = 64
PAST_LEN = 4096

CHUNK = 64
N_PREV_CHUNKS = 8
BAND_ROWS = N_PREV_CHUNKS * CHUNK
D_MIX = D_MODEL
D_ATTN = D_MIX // 2
D_MLSTM = D_MIX - D_ATTN
HEAD_DIM_A = 64
N_HEADS_A = D_ATTN // HEAD_DIM_A
HEAD_DIM_B = 128
N_HEADS_B = D_MLSTM // HEAD_DIM_B
REL_CLIP = 256
CONV_W = 4
D_FF = -(-8 * D_MODEL // (3 * 256)) * 256
D_IN_PROJ = 3 * D_ATTN + 4 * D_MLSTM + 2 * N_HEADS_B
DN_ALPHA = (2.0 * DEPTH) ** 0.25
DN_BETA = (8.0 * DEPTH) ** -0.25

kernel_name = 'hybrid_chunkband_mlstm_stream_step'


def layer_norm(x, g, b, eps=1e-5):
    xf = x.astype(jnp.float32)
    mu = jnp.mean(xf, axis=-1, keepdims=True)
    var = jnp.mean(jnp.square(xf - mu), axis=-1, keepdims=True)
    return ((xf - mu) * lax.rsqrt(var + eps) * g + b).astype(x.dtype)


def head_rms_norm(h, g, eps=1e-6):
    hf = h.astype(jnp.float32)
    hf = hf * lax.rsqrt(jnp.mean(hf * hf, axis=-1, keepdims=True) + eps)
    return (hf * g.reshape(h.shape[-2], h.shape[-1])).astype(h.dtype)


def split_in_proj(z):
    sizes = [D_ATTN, D_ATTN, D_ATTN, 2 * D_MLSTM, D_MLSTM, D_MLSTM, N_HEADS_B, N_HEADS_B]
    return jnp.split(z, [int(s) for s in np.cumsum(sizes)[:-1]], axis=-1)


def attend(q, k, v, bias, valid):
    s = jnp.einsum('bqhd,bkhd->bhqk', q, k).astype(jnp.float32) * (HEAD_DIM_A ** -0.5)
    s = s + bias[None].astype(jnp.float32)
    if valid is not None:
        s = jnp.where(valid[None, None, None, :], s, -jnp.inf)
    p = jax.nn.softmax(s, axis=-1)
    return jnp.einsum('bhqk,bkhd->bqhd', p.astype(v.dtype), v)


def chunk_band_attention_prompt(q, k, v, rel_bias):
    B, S, H, Dh = q.shape
    n_chunks = S // CHUNK
    band = BAND_ROWS + CHUNK
    pad = jnp.zeros((B, BAND_ROWS, H, Dh), k.dtype)
    k_pad = jnp.concatenate([pad, k], axis=1)
    v_pad = jnp.concatenate([pad, v], axis=1)
    rel = jnp.arange(CHUNK)[:, None] + BAND_ROWS - jnp.arange(band)[None, :]
    bias = rel_bias[:, jnp.clip(rel, -REL_CLIP, REL_CLIP) + REL_CLIP]

    def one_chunk(c):
        start = c * CHUNK
        qc = lax.dynamic_slice_in_dim(q, start, CHUNK, axis=1)
        kc = lax.dynamic_slice_in_dim(k_pad, start, band, axis=1)
        vc = lax.dynamic_slice_in_dim(v_pad, start, band, axis=1)
        valid = (start - BAND_ROWS + jnp.arange(band)) >= 0
        return attend(qc, kc, vc, bias, valid)

    out = lax.map(one_chunk, jnp.arange(n_chunks))
    out = jnp.moveaxis(out, 0, 1).reshape(B, S, H, Dh)
    n_keep = min(BAND_ROWS, S)
    return out, k[:, S - n_keep:], v[:, S - n_keep:]


def chunk_band_attention_sample(q, k, v, cache_k, cache_v, rel_bias):
    B, L, H, Dh = q.shape
    n_past = cache_k.shape[1]
    k_all = jnp.concatenate([cache_k.astype(k.dtype), k], axis=1)
    v_all = jnp.concatenate([cache_v.astype(v.dtype), v], axis=1)
    key_off = jnp.concatenate([jnp.arange(n_past) - n_past, jnp.arange(L)])
    rel = jnp.arange(L)[:, None] - key_off[None, :]
    bias = rel_bias[:, jnp.clip(rel, -REL_CLIP, REL_CLIP) + REL_CLIP]
    out = attend(q, k_all, v_all, bias, None)
    n_keep = min(BAND_ROWS, n_past + L)
    return out, k_all[:, n_past + L - n_keep:], v_all[:, n_past + L - n_keep:]


def mlstm_chunk(state, inputs):
    C, n, m = state
    q, k, v, ig, lf = inputs
    L = q.shape[1]
    F = jnp.cumsum(lf, axis=1)
    a = F + m[:, None, :]
    causal = jnp.tril(jnp.ones((L, L), dtype=bool))
    d = jnp.where(causal[None, :, :, None],
                  F[:, :, None, :] - F[:, None, :, :] + ig[:, None, :, :], -jnp.inf)
    m_t = jnp.maximum(a, jnp.max(d, axis=2))
    w_inter = jnp.exp(a - m_t)
    s = jnp.einsum('bthd,bjhd->btjh', q, k) * jnp.exp(d - m_t[:, :, None, :])
    num = jnp.einsum('btjh,bjhe->bthe', s, v) + w_inter[..., None] * jnp.einsum('bthd,bhde->bthe', q, C)
    den = jnp.sum(s, axis=2) + w_inter * jnp.einsum('bthd,bhd->bth', q, n)
    h = num / jnp.maximum(jnp.abs(den), jnp.exp(-m_t))[..., None]
    F_L = F[:, -1]
    g = F_L[:, None, :] - F + ig
    a_L = F_L + m
    m_new = jnp.maximum(a_L, jnp.max(g, axis=1))
    w_old = jnp.exp(a_L - m_new)
    w_k = jnp.exp(g - m_new[:, None, :])
    C_new = w_old[..., None, None] * C + jnp.einsum('bjh,bjhd,bjhe->bhde', w_k, k, v)
    n_new = w_old[..., None] * n + jnp.einsum('bjh,bjhd->bhd', w_k, k)
    return (C_new, n_new, m_new), h


def mlstm_mixer(qk_raw, v, i_pre, f_pre, conv_buf, state, w_conv, b_conv, b_igate, b_fgate):
    B, L, _ = qk_raw.shape
    qk_pad = jnp.concatenate([conv_buf.astype(qk_raw.dtype), qk_raw], axis=1)
    qk = b_conv + sum(qk_pad[:, j:j + L] * w_conv[j] for j in range(CONV_W))
    qk = jax.nn.silu(qk).astype(jnp.float32)
    q = qk[..., :D_MLSTM].reshape(B, L, N_HEADS_B, HEAD_DIM_B)
    k = qk[..., D_MLSTM:].reshape(B, L, N_HEADS_B, HEAD_DIM_B) * (HEAD_DIM_B ** -0.5)
    vf = v.astype(jnp.float32).reshape(B, L, N_HEADS_B, HEAD_DIM_B)
    ig = (i_pre + b_igate).astype(jnp.float32)
    lf = jax.nn.log_sigmoid((f_pre + b_fgate).astype(jnp.float32))
    state = tuple(s.astype(jnp.float32) for s in state)
    if L > CHUNK:
        n_chunks = L // CHUNK
        to_chunks = lambda t: jnp.moveaxis(t.reshape(B, n_chunks, CHUNK, *t.shape[2:]), 1, 0)
        new_state, h = lax.scan(mlstm_chunk, state, tuple(to_chunks(t) for t in (q, k, vf, ig, lf)))
        h = jnp.moveaxis(h, 0, 1).reshape(B, L, N_HEADS_B, HEAD_DIM_B)
    else:
        new_state, h = mlstm_chunk(state, (q, k, vf, ig, lf))
    return h, new_state, qk_pad[:, L:]


def trunk_layer(x, attention, conv_buf, mlstm_state, w_in, b_igate, b_fgate, w_conv, b_conv,
                g_attn_norm, g_mlstm_norm, w_out, ln1_g, ln1_b, w_ffn_gate, w_ffn_up, w_ffn_down,
                ln2_g, ln2_b):
    B, L, _ = x.shape
    z = x @ w_in
    qa, ka, va, qk_raw, vb, ob, ib, fb = split_in_proj(z)
    to_heads = lambda t: t.reshape(B, L, N_HEADS_A, HEAD_DIM_A)
    attn, new_k, new_v = attention(to_heads(qa), to_heads(ka), to_heads(va))
    h_b, new_state, new_conv = mlstm_mixer(qk_raw, vb, ib, fb, conv_buf, mlstm_state,
                                           w_conv, b_conv, b_igate, b_fgate)
    attn_out = head_rms_norm(attn, g_attn_norm).reshape(B, L, D_ATTN)
    mlstm_out = (head_rms_norm(h_b, g_mlstm_norm).reshape(B, L, D_MLSTM)
                 * jax.nn.sigmoid(ob)).astype(x.dtype)
    mix = jnp.concatenate([attn_out, mlstm_out], axis=-1) @ w_out
    x1 = layer_norm(DN_ALPHA * x + mix, ln1_g, ln1_b)
    ffn = (jax.nn.silu(x1 @ w_ffn_gate) * (x1 @ w_ffn_up)) @ w_ffn_down
    y = layer_norm(DN_ALPHA * x1 + ffn, ln2_g, ln2_b)
    return y, new_k, new_v, new_conv, new_state


def setup_inputs(seed: int = 0) -> dict:
    key = jax.random.key(seed)
    ks = jax.random.split(key, 24)
    nrm = lambda k, shape, scale: scale * jax.random.normal(k, shape, jnp.float32)
    band = min(BAND_ROWS, PAST_LEN)
    return {
        'x_prompt': nrm(ks[0], (BATCH, SEQ, D_MODEL), 1.0),
        'x_sample': nrm(ks[1], (DEC_BATCH, DEC_SEQ, D_MODEL), 1.0),
        'cache_attn_k': nrm(ks[2], (DEPTH, DEC_BATCH, band, N_HEADS_A, HEAD_DIM_A), 1.0),
        'cache_attn_v': nrm(ks[3], (DEPTH, DEC_BATCH, band, N_HEADS_A, HEAD_DIM_A), 1.0),
        'state_conv': nrm(ks[4], (DEPTH, DEC_BATCH, CONV_W - 1, 2 * D_MLSTM), 1.0),
        'state_mlstm_C': nrm(ks[5], (DEPTH, DEC_BATCH, N_HEADS_B, HEAD_DIM_B, HEAD_DIM_B), 0.3),
        'state_mlstm_n': nrm(ks[6], (DEPTH, DEC_BATCH, N_HEADS_B, HEAD_DIM_B), 0.3),
        'state_mlstm_m': nrm(ks[7], (DEPTH, DEC_BATCH, N_HEADS_B), 1.0),
        'w_in': nrm(ks[8], (DEPTH, D_MODEL, D_IN_PROJ), D_MODEL ** -0.5),
        'b_igate': nrm(ks[9], (DEPTH, N_HEADS_B), 0.5),
        'b_fgate': 3.0 + nrm(ks[10], (DEPTH, N_HEADS_B), 0.5),
        'w_conv': nrm(ks[11], (DEPTH, CONV_W, 2 * D_MLSTM), CONV_W ** -0.5),
        'b_conv': nrm(ks[12], (DEPTH, 2 * D_MLSTM), 0.01),
        'rel_bias': nrm(ks[13], (DEPTH, N_HEADS_A, 2 * REL_CLIP + 1), 0.5),
        'g_attn_norm': 1.0 + nrm(ks[14], (DEPTH, D_ATTN), 0.01),
        'g_mlstm_norm': 1.0 + nrm(ks[15], (DEPTH, D_MLSTM), 0.01),
        'w_out': nrm(ks[16], (DEPTH, D_MIX, D_MODEL), DN_BETA * D_MIX ** -0.5),
        'ln1_g': 1.0 + nrm(ks[17], (DEPTH, D_MODEL), 0.01),
        'ln1_b': nrm(ks[18], (DEPTH, D_MODEL), 0.01),
        'w_ffn_gate': nrm(ks[19], (DEPTH, D_MODEL, D_FF), D_MODEL ** -0.5),
        'w_ffn_up': nrm(ks[20], (DEPTH, D_MODEL, D_FF), D_MODEL ** -0.5),
        'w_ffn_down': nrm(ks[21], (DEPTH, D_FF, D_MODEL), DN_BETA * D_FF ** -0.5),
        'ln2_g': 1.0 + nrm(ks[22], (DEPTH, D_MODEL), 0.01),
        'ln2_b': nrm(ks[23], (DEPTH, D_MODEL), 0.01),
    }


def reference(x_prompt, x_sample, cache_attn_k, cache_attn_v, state_conv, state_mlstm_C,
              state_mlstm_n, state_mlstm_m, w_in, b_igate, b_fgate, w_conv, b_conv, rel_bias,
              g_attn_norm, g_mlstm_norm, w_out, ln1_g, ln1_b, w_ffn_gate, w_ffn_up, w_ffn_down,
              ln2_g, ln2_b):
    xp, xs = x_prompt, x_sample
    bp = x_prompt.shape[0]
    new_p, new_s = [], []
    for l in range(DEPTH):
        weights = (w_in[l], b_igate[l], b_fgate[l], w_conv[l], b_conv[l], g_attn_norm[l],
                   g_mlstm_norm[l], w_out[l], ln1_g[l], ln1_b[l], w_ffn_gate[l], w_ffn_up[l],
                   w_ffn_down[l], ln2_g[l], ln2_b[l])
        zero_conv = jnp.zeros((bp, CONV_W - 1, 2 * D_MLSTM), xp.dtype)
        zero_state = (jnp.zeros((bp, N_HEADS_B, HEAD_DIM_B, HEAD_DIM_B), jnp.float32),
                      jnp.zeros((bp, N_HEADS_B, HEAD_DIM_B), jnp.float32),
                      jnp.zeros((bp, N_HEADS_B), jnp.float32))
        attn_p = functools.partial(chunk_band_attention_prompt, rel_bias=rel_bias[l])
        xp, kp, vp, cp, (Cp, n_p, mp) = trunk_layer(xp, attn_p, zero_conv, zero_state, *weights)
        new_p.append((kp, vp, cp, Cp, n_p, mp))
        attn_s = functools.partial(chunk_band_attention_sample, cache_k=cache_attn_k[l],
                                   cache_v=cache_attn_v[l], rel_bias=rel_bias[l])
        xs, ks_, vs, cs, (Cs, n_s, ms) = trunk_layer(
            xs, attn_s, state_conv[l], (state_mlstm_C[l], state_mlstm_n[l], state_mlstm_m[l]), *weights)
        new_s.append((ks_, vs, cs, Cs, n_s, ms))
    k_p, v_p, conv_p, C_p, n_p, m_p = [jnp.stack(t) for t in zip(*new_p)]
    k_s, v_s, conv_s, C_s, n_s, m_s = [jnp.stack(t) for t in zip(*new_s)]
    return (xp, xs, k_p, v_p, conv_p, C_p, n_p, m_p, k_s, v_s, conv_s, C_s, n_s, m_s)
```

```python
import numpy as np
from contextlib import ExitStack
import concourse.bass as bass
import concourse.mybir as mybir
from concourse.bass_utils import run_bass_kernel_spmd

F32 = mybir.dt.float32
BF16 = mybir.dt.bfloat16
AF = mybir.ActivationFunctionType
ALU = mybir.AluOpType

D = 2048
DIN = 7184
DFF = 5632
ALPHA = 2.0 ** 0.25
KSCALE = 128.0 ** -0.5


class Tok:
    __slots__ = ("w", "r", "excl")

    def __init__(self, excl=False):
        self.w = None
        self.r = {}
        self.excl = excl


class Sched:
    NDMA = 40

    def __init__(self, nc, es):
        self.nc = nc
        self.es = es
        self.eng = {'pe': nc.tensor, 'act': nc.scalar, 'dve': nc.vector, 'pool': nc.gpsimd, 'sp': nc.sync}
        self.sem = {k: es.enter_context(nc.semaphore("s_" + k)) for k in ('pe', 'act', 'dve', 'pool')}
        self.cnt = {k: 0 for k in self.sem}
        self.seen = {k: {} for k in self.eng}
        self.dsem = [es.enter_context(nc.semaphore("d%d" % i)) for i in range(self.NDMA)]
        self.dtot = [0] * self.NDMA
        self.dnext = 0
        self.pending = {k: [] for k in self.eng}

    def sbuf(self, name, shape, dtype):
        return self.es.enter_context(self.nc.sbuf_tensor(name, shape, dtype))

    def psum(self, name, shape, dtype):
        return self.es.enter_context(self.nc.psum_tensor(name, shape, dtype))

    def _wait(self, eng, ev):
        key, sem, val = ev
        if self.seen[eng].get(key, 0) >= val:
            return
        self.eng[eng].wait_ge(sem, val)
        self.seen[eng][key] = val

    def _deps(self, eng, r, w):
        for t in r:
            if t.w is not None:
                if t.w[0] == eng and eng == 'pe':
                    continue
                self._wait(eng, t.w)
            if t.excl:
                for ev in list(t.r.values()):
                    if ev[0] != eng:
                        self._wait(eng, ev)
        for t in w:
            if t.w is not None and not (t.w[0] == eng and eng != 'pool'):
                self._wait(eng, t.w)
            for ev in t.r.values():
                if ev[0] == eng and eng != 'pool':
                    continue
                self._wait(eng, ev)

    def _register(self, ev, r, w):
        for t in r:
            old = t.r.get(ev[0])
            if old is None or old[2] < ev[2]:
                t.r[ev[0]] = ev
        for t in w:
            t.w = ev
            t.r = {}

    def op(self, eng, fn, r=(), w=(), inc=True):
        self._deps(eng, r, w)
        inst = fn(self.eng[eng])
        if not inc:
            self.pending[eng].append((tuple(r), tuple(w)))
            return None
        self.cnt[eng] += 1
        inst.then_inc(self.sem[eng], 1)
        ev = (eng, self.sem[eng], self.cnt[eng])
        for pr, pw in self.pending[eng]:
            self._register(ev, pr, pw)
        self.pending[eng] = []
        self._register(ev, r, w)
        return ev

    def dma(self, q, out, in_, r=(), w=(), **kw):
        lo, hi = (0, 24) if q == 'sp' else (24, self.NDMA)
        if not hasattr(self, 'qn'):
            self.qn = {}
        i = self.qn.get(q, lo)
        self.qn[q] = lo + (i + 1 - lo) % (hi - lo)
        self._deps(q, r, w)
        key = 'd%d' % i
        if self.dtot[i] > 0:
            self._wait(q, (key, self.dsem[i], self.dtot[i]))
        inst = self.eng[q].dma_start(out=out, in_=in_, **kw)
        self.dtot[i] += 16
        inst.then_inc(self.dsem[i], 16)
        ev = (key, self.dsem[i], self.dtot[i])
        self._register(ev, r, w)
        return ev

    def barrier(self):
        for e in ('pe', 'act', 'dve', 'pool', 'sp'):
            for e2 in ('pe', 'act', 'dve', 'pool'):
                if e2 != e and self.cnt[e2] > 0:
                    self._wait(e, (e2, self.sem[e2], self.cnt[e2]))
            for i in range(self.NDMA):
                if self.dtot[i] > 0:
                    self._wait(e, ('d%d' % i, self.dsem[i], self.dtot[i]))

    def barrier_pe(self):
        if self.cnt['pe'] > 0:
            for e in ('act', 'dve', 'pool', 'sp'):
                self._wait(e, ('pe', self.sem['pe'], self.cnt['pe']))

    def finish(self):
        self.barrier()


class Buf:
    def __init__(self, ap):
        self.a = ap
        self.t = Tok()


def build(NSEQ=2, SEQ=2048, SAMPLE=True, DBG=False):
    nc = bass.Bass("TRN2", target_bir_lowering=False)
    NT = SEQ // 512

    def din(name, shape):
        return nc.dram_tensor(name, shape, F32, kind="ExternalInput").ap()

    def dout(name, shape):
        return nc.dram_tensor(name, shape, F32, kind="ExternalOutput").ap()

    def dint(name, shape, dt):
        return nc.dram_tensor(name, shape, dt, kind="Internal").ap()

    xp = din("xp", [NSEQ * SEQ, D])
    xs = din("xs", [64, D])
    ck = din("ck", [512, 1024])
    cv = din("cv", [512, 1024])
    sconv = din("sconv", [3, 2048])
    sC = din("sC", [8, 128, 128])
    sn = din("sn", [8, 128])
    sm = din("sm", [8, 1])
    w_in = din("w_in", [D, DIN])
    b_ig = din("b_ig", [8, 1])
    b_fg = din("b_fg", [8, 1])
    w_conv = din("w_conv", [4, 2048])
    b_conv = din("b_conv", [1, 2048])
    rel_bias = din("rel_bias", [16, 513])
    g_attn = din("g_attn", [1, 1024])
    g_mlstm = din("g_mlstm", [1, 1024])
    w_out = din("w_out", [D, D])
    ln1_g = din("ln1_g", [1, D])
    ln1_b = din("ln1_b", [1, D])
    w_g = din("w_g", [D, DFF])
    w_u = din("w_u", [D, DFF])
    w_d = din("w_d", [DFF, D])
    ln2_g = din("ln2_g", [1, D])
    ln2_b = din("ln2_b", [1, D])

    yp = dout("yp", [NSEQ * SEQ, D])
    ys = dout("ys", [64, D])
    kp = dout("kp", [NSEQ * 512, 1024])
    vp = dout("vp", [NSEQ * 512, 1024])
    convp = dout("convp", [NSEQ * 3, 2048])
    Cp = dout("Cp", [NSEQ * 8, 128, 128])
    np_ = dout("np", [NSEQ * 8, 128])
    mp = dout("mp", [NSEQ * 8, 1])
    ks = dout("ks", [512, 1024])
    vs = dout("vs", [512, 1024])
    convs = dout("convs", [3, 2048])
    Cs = dout("Cs", [8, 128, 128])
    ns = dout("ns", [8, 128])
    ms = dout("ms", [8, 1])

    wbi = dint("wbi", [D, DIN], BF16)
    wbo = dint("wbo", [D, D], BF16)
    wbg = dint("wbg", [D, DFF], BF16)
    wbu = dint("wbu", [D, DFF], BF16)
    wbd = dint("wbd", [DFF, D], BF16)
    extd = dint("extd", [16, 768], F32)
    tzd = dint("tzd", [16, 128, 640], F32)
    mixd = dint("mixd", [512, D], F32)
    x1d = dint("x1d", [512, D], F32)
    ypd = dint("ypd", [512, D], F32)

    es = ExitStack()
    with es:
        S = Sched(nc, es)

        def sb(name, shape, dt=F32):
            return Buf(S.sbuf(name, shape, dt)[:])

        bank = [S.psum("pb%d" % i, [128, 512], F32)[:] for i in range(8)]
        PT = [Tok(excl=True) for _ in range(8)]
        bctr = [0]

        def nb():
            bctr[0] = (bctr[0] + 1) % 8
            return bctr[0]

        ident_f = sb("ident_f", [128, 128])
        ident_b = sb("ident_b", [128, 128], BF16)
        Jm = sb("Jm", [128, 128])
        mask64 = sb("mask64", [128, 64])
        selall = sb("selall", [8, 8 * 128])
        rm = sb("rm", [8, 512])
        nm = sb("nm", [8, 512])
        ones_bf = sb("ones_bf", [128, 64], BF16)
        S.op('pool', lambda e: e.memset(ident_f.a, 0.0), w=[ident_f.t])
        S.op('pool', lambda e: e.affine_select(out=ident_f.a, in_=ident_f.a, compare_op=ALU.not_equal, fill=1.0,
                                                base=0, pattern=[[-1, 128]], channel_multiplier=1),
             r=[ident_f.t], w=[ident_f.t])
        S.op('dve', lambda e: e.tensor_copy(ident_b.a, ident_f.a), r=[ident_f.t], w=[ident_b.t])
        S.op('pool', lambda e: e.memset(Jm.a, 0.0), w=[Jm.t])
        S.op('pool', lambda e: e.affine_select(out=Jm.a, in_=Jm.a, compare_op=ALU.not_equal, fill=1.0,
                                                base=-127, pattern=[[1, 128]], channel_multiplier=1),
             r=[Jm.t], w=[Jm.t])
        S.op('pool', lambda e: e.memset(mask64.a, 1.0), w=[mask64.t])
        S.op('pool', lambda e: e.affine_select(out=mask64.a[0:64, :], in_=mask64.a[0:64, :], compare_op=ALU.is_ge,
                                                fill=0.0, base=0, pattern=[[1, 64]], channel_multiplier=-1),
             r=[mask64.t], w=[mask64.t])
        S.dma('sp', mask64.a[64:128, :], mask64.a[0:64, :], r=[mask64.t], w=[mask64.t])
        S.op('pool', lambda e: e.memset(selall.a, 0.0), w=[selall.t])
        S.op('pool', lambda e: e.affine_select(out=selall.a.rearrange("p (h m) -> p h m", m=128),
                                                in_=selall.a.rearrange("p (h m) -> p h m", m=128),
                                                compare_op=ALU.not_equal, fill=1.0, base=0,
                                                pattern=[[-1, 8], [0, 128]], channel_multiplier=1),
             r=[selall.t], w=[selall.t])
        S.op('pool', lambda e: e.memset(rm.a, 1.0), w=[rm.t])
        S.op('pool', lambda e: e.memset(rm.a.rearrange("p (c t) -> p c t", t=64)[:, :, 0:1], 0.0), r=[rm.t], w=[rm.t])
        S.op('pool', lambda e: e.memset(nm.a, 0.0), w=[nm.t])
        S.op('pool', lambda e: e.memset(nm.a.rearrange("p (c t) -> p c t", t=64)[:, :, 0:1], -1e30), r=[nm.t], w=[nm.t])
        S.op('pool', lambda e: e.memset(ones_bf.a, 1.0), w=[ones_bf.t])
        c64 = sb("c64", [128, 64], BF16)
        S.op('pool', lambda e: e.memset(c64.a, 1.0 / 64), w=[c64.t])
        eps6 = sb("eps6", [128, 1])
        S.op('pool', lambda e: e.memset(eps6.a, 1e-6), w=[eps6.t])

        def cast(dst, src, rows, nslab):
            toks = []
            step = rows // nslab
            for i in range(nslab):
                t = Tok()
                S.dma('pool', dst[i * step:(i + 1) * step, :], src[i * step:(i + 1) * step, :], w=[t])
                toks.append(t)
            return toks

        def cast_cols(dst, src, rows, nslab, c0, c1):
            toks = []
            step = rows // nslab
            for i in range(nslab):
                t = Tok()
                S.dma('pool', dst[i * step:(i + 1) * step, c0:c1], src[i * step:(i + 1) * step, c0:c1], w=[t])
                toks.append(t)
            return toks

        t_wbiA = cast_cols(wbi, w_in, D, 32, 0, 3072)
        t_wbiB = cast_cols(wbi, w_in, D, 32, 3072, DIN)
        t_wbi = t_wbiB

        wtail = sb("wtail", [128, 16, 16], BF16)
        wconvT = sb("wconvT", [128, 4, 16])
        bconvT = sb("bconvT", [128, 16])
        gAT = sb("gAT", [128, 8])
        gBT = sb("gBT", [128, 8])
        big = sb("big", [8, 1])
        bfg = sb("bfg", [8, 1])
        nbf = sb("nbf", [8, 1])
        S.dma('sp', wtail.a, wbi[:, 7168:7184].rearrange("(kc p) n -> p kc n", p=128), r=t_wbi, w=[wtail.t])
        for j_ in range(4):
            S.dma('sp', wconvT.a[:, j_, :], w_conv[j_:j_ + 1, :].rearrange("o (fc p) -> p (o fc)", p=128), w=[wconvT.t],
                  allow_slow_non_contiguous=True)
        S.dma('sp', bconvT.a, b_conv.rearrange("o (fc p) -> p (o fc)", p=128), w=[bconvT.t], allow_slow_non_contiguous=True)
        S.dma('sp', gAT.a, g_attn.rearrange("o (c p) -> p (o c)", p=128), w=[gAT.t], allow_slow_non_contiguous=True)
        S.dma('sp', gBT.a, g_mlstm.rearrange("o (c p) -> p (o c)", p=128), w=[gBT.t], allow_slow_non_contiguous=True)
        S.dma('sp', big.a, b_ig, w=[big.t])
        S.dma('sp', bfg.a, b_fg, w=[bfg.t])
        S.op('dve', lambda e: e.tensor_scalar(nbf.a, bfg.a, -1.0, None, ALU.mult), r=[bfg.t], w=[nbf.t])

        kT = sb("kT", [128, 8, 1024], BF16)
        vA = sb("vA", [128, 8, 16, 64], BF16)
        hist = sb("hist", [128, 3, 16])
        Caug = sb("Caug", [128, 8, 129])
        Csbf = sb("Csbf", [128, 8, 129], BF16)
        mprev = sb("mprev", [8, 1])
        bc = sb("bc", [128, 8, 24])
        xT = sb("xT", [128, 16, 512], BF16)
        wbuf = [sb("wbuf%d" % i, [128, 16, 512], BF16) for i in range(2)]
        concatT = sb("concatT", [128, 16, 512], BF16)
        xs_st = sb("xs_st", [128, 2048])
        smalls = sb("smalls", [128, 128])
        smt_l = [Tok() for _ in range(4)]
        smt2 = [Tok() for _ in range(4)]
        gsm = sb("gsm", [8, 128])

        ARENA_B = 80 * 1024
        arena = S.sbuf("arena", [128, ARENA_B // 2], BF16)[:]

        class Carver:
            def __init__(self):
                self.off = 0

            def take(self, shape, dt):
                n = 1
                for s_ in shape[1:]:
                    n *= s_
                nbytes = n * (4 if dt == F32 else 2)
                nbytes = (nbytes + 63) // 64 * 64
                a = arena[:, self.off // 2:(self.off + nbytes) // 2]
                self.off += nbytes
                assert self.off <= ARENA_B, self.off
                if dt == F32:
                    a = a.bitcast(F32)
                a = a[:, 0:n]
                if len(shape) == 3:
                    a = a.rearrange("p (a b) -> p a b", b=shape[2])
                elif len(shape) == 4:
                    a = a.rearrange("p (a b c) -> p a b c", b=shape[2], c=shape[3])
                return Buf(a[0:shape[0]])

        cv_ = Carver()
        qT = cv_.take([128, 8, 512], BF16)
        tzb2 = [cv_.take([128, 640], F32) for _ in range(2)]
        tzb = tzb2[0]
        sbt = [cv_.take([128, 512], F32) for _ in range(3)]
        pT = [cv_.take([128, 512], BF16) for _ in range(5)]
        ocp = [cv_.take([128, 512], F32) for _ in range(2)]
        osq = [cv_.take([128, 512], BF16) for _ in range(2)]
        att_t = [cv_.take([128, 512], F32) for _ in range(2)]
        att_r = [cv_.take([128, 512], F32) for _ in range(2)]
        kst = [cv_.take([128, 512], F32) for _ in range(2)]
        cv_ = Carver()
        qmT = cv_.take([128, 8, 512], BF16)
        kmT = cv_.take([128, 8, 512], BF16)
        gsT = cv_.take([128, 8, 512], F32)
        vB = cv_.take([128, 4, 8, 129], BF16)
        kTok = cv_.take([128, 4, 8, 128], BF16)
        raw = [cv_.take([128, 515], F32) for _ in range(2)]
        acc = cv_.take([128, 512], F32)
        sigt = cv_.take([128, 512], F32)
        G = [cv_.take([8, 512], F32) for _ in range(4)]
        eu_tok = cv_.take([128, 4, 8], F32)
        enf_tok = cv_.take([128, 4, 8], F32)
        Pm = [cv_.take([128, 64], BF16) for _ in range(6)]
        vpm = [cv_.take([128, 129], BF16) for _ in range(6)]
        updt = [cv_.take([128, 129], F32) for _ in range(6)]
        hn = [cv_.take([128, 128], F32) for _ in range(6)]
        sml = [cv_.take([128, 8], F32) for _ in range(6)]
        junk = cv_.take([128, 128], F32)
        cv_ = Carver()
        hT = cv_.take([128, 44, 512], BF16)
        lnA = cv_.take([128, 2048], F32)
        lnG = cv_.take([128, 2048], F32)
        lnB = cv_.take([128, 2048], F32)
        ostage = [cv_.take([128, 512], F32) for _ in range(2)]
        sg4 = [cv_.take([128, 512], F32) for _ in range(4)]

        t_wbo = cast(wbo, w_out, D, 32)
        t_wbg = cast(wbg, w_g, D, 64)
        t_wbu = cast(wbu, w_u, D, 64)
        t_wbd = cast(wbd, w_d, DFF, 88)

        t_tzd = Tok()
        rbs = tzb
        extS = sbt[1]
        S.dma('sp', rbs.a[0:16, 0:513], rel_bias, w=[rbs.t])
        S.op('dve', lambda e: e.tensor_copy(extS.a[0:16, 0:384], rbs.a[0:16, 129:513]), r=[rbs.t], w=[extS.t])
        S.op('dve', lambda e: e.tensor_scalar(att_t[0].a[0:16, 0:384], rbs.a[0:16, 0:384], 0.0, rbs.a[0:16, 512:513],
                                               ALU.mult, ALU.add), r=[rbs.t], w=[att_t[0].t])
        t_ext = Tok()
        S.dma('sp', extd[:, 0:384], extS.a[0:16, 0:384], r=[extS.t], w=[t_ext])
        S.dma('sp', extd[:, 384:768], att_t[0].a[0:16, 0:384], r=[att_t[0].t, t_ext], w=[t_ext])
        for h in range(16):
            src = bass.AP(tensor=extd.tensor, offset=h * 768, ap=[[1, 128], [1, 640]])
            S.dma('sp', tzb.a, src, r=[t_ext], w=[tzb.t])
            b0 = nb()
            S.op('pe', lambda e: e.matmul(bank[b0][:, 0:512], Jm.a, tzb.a[:, 0:512], start=True, stop=True),
                 r=[Jm.t, tzb.t], w=[PT[b0]])
            S.op('dve', lambda e: e.tensor_copy(ocp[h % 2].a[:, 0:512], bank[b0][:, 0:512]), r=[PT[b0]], w=[ocp[h % 2].t])
            b1 = nb()
            S.op('pe', lambda e: e.matmul(bank[b1][:, 0:128], Jm.a, tzb.a[:, 512:640], start=True, stop=True),
                 r=[Jm.t, tzb.t], w=[PT[b1]])
            S.op('act', lambda e: e.copy(att_t[h % 2].a[:, 0:128], bank[b1][:, 0:128]), r=[PT[b1]], w=[att_t[h % 2].t])
            S.dma('pool', tzd[h, :, 0:512], ocp[h % 2].a[:, 0:512], r=[ocp[h % 2].t], w=[t_tzd])
            S.dma('pool', tzd[h, :, 512:640], att_t[h % 2].a[:, 0:128], r=[att_t[h % 2].t], w=[t_tzd])
        S.barrier()

        evtog = [0]

        def evac_copy(out, in_, r, w):
            evtog[0] ^= 1
            if evtog[0]:
                S.op('act', lambda e: e.copy(out, in_), r=r, w=w)
            else:
                S.op('dve', lambda e: e.tensor_copy(out, in_), r=r, w=w)

        def accum(b, M, N, nk, lhs_fn, rhs_fn, r, start=True, stop=True):
            for kc in range(nk):
                S.op('pe', lambda e: e.matmul(bank[b][0:M, 0:N], lhs_fn(kc), rhs_fn(kc),
                                               start=(start and kc == 0), stop=(stop and kc == nk - 1)),
                     r=r, w=[PT[b]], inc=(kc == nk - 1))

        items = []

        def add(loads, run):
            items.append((loads, run))

        def wload(dram, k0, nk, c0, ncols, toks, dc0=0):
            return (dram, k0, nk, c0, ncols, toks, dc0)

        cflat = concatT.a.rearrange("p a b -> p (a b)")
        lnA2 = Buf(cflat[:, 0:4096].bitcast(F32))
        xs2 = Buf(cflat[:, 4096:8192].bitcast(F32))

        def ln_pass(T, src_scr, src_tok, res_src_fn, res_tok, g_d, b_d, dst_fn, dst_tok, make_xT):
            RP = min(128, T)
            NSB = (T + 127) // 128
            S.dma('sp', lnG.a, g_d.partition_broadcast(128), w=[lnG.t])
            S.dma('sp', lnB.a, b_d.partition_broadcast(128), w=[lnB.t])
            for sbi in range(NSB):
                r0 = sbi * 128
                if sbi % 2 == 0:
                    A, X, at, xt_ = lnA, xs_st, [lnA.t], [xs_st.t]
                else:
                    A, X, at, xt_ = lnA2, xs2, [lnA2.t, concatT.t], [xs2.t, concatT.t]
                st = smalls.a[0:RP, sbi * 32:sbi * 32 + 24]
                mv = smalls.a[0:RP, sbi * 32 + 24:sbi * 32 + 26]
                rstd = smalls.a[0:RP, sbi * 32 + 26:sbi * 32 + 27]
                nmr = smalls.a[0:RP, sbi * 32 + 27:sbi * 32 + 28]
                smt = smt_l[sbi]
                S.dma('sp', A.a[0:RP, :], src_scr[r0:r0 + RP, :], r=[src_tok], w=at)
                S.dma('sp', X.a[0:RP, :], res_src_fn(sbi), r=[res_tok], w=xt_)
                S.op('dve', lambda e: e.scalar_tensor_tensor(A.a[0:RP, :], X.a[0:RP, :], ALPHA, A.a[0:RP, :],
                                                              ALU.mult, ALU.add), r=xt_ + at, w=at)
                for q in range(4):
                    S.op('dve', lambda e: e.bn_stats(st[:, q * 6:(q + 1) * 6], A.a[0:RP, q * 512:(q + 1) * 512]),
                         r=at, w=[smt])
                S.op('dve', lambda e: e.bn_aggr(mv, st), r=[smt], w=[smt, smt2[sbi]])
                S.op('act', lambda e: e.activation(rstd, mv[:, 1:2], AF.Sqrt, bias=1e-5, scale=1.0), r=[smt], w=[smt])
                S.op('dve', lambda e: e.scalar_tensor_tensor(A.a[0:RP, :], A.a[0:RP, :], mv[:, 0:1], lnG.a[0:RP, :],
                                                              ALU.subtract, ALU.mult), r=[smt2[sbi], lnG.t] + at, w=at)
                S.op('dve', lambda e: e.reciprocal(rstd, rstd), r=[smt], w=[smt])
                S.op('dve', lambda e: e.scalar_tensor_tensor(A.a[0:RP, :], A.a[0:RP, :], rstd, lnB.a[0:RP, :],
                                                              ALU.mult, ALU.add), r=[smt, lnB.t] + at, w=at)
                S.dma('pool', dst_fn(sbi), A.a[0:RP, :], r=at, w=[dst_tok])
                if make_xT:
                    for q in range(4):
                        b = nb()
                        for j in range(4):
                            kc = q * 4 + j
                            S.op('pe', lambda e: e.transpose(bank[b][:, j * 128:j * 128 + RP],
                                                              A.a[0:RP, kc * 128:(kc + 1) * 128],
                                                              ident_f.a[0:RP, 0:RP]),
                                 r=at + [ident_f.t], w=[PT[b]], inc=(j == 3))
                        evac_copy(xT.a[:, q * 4:(q + 1) * 4, r0:r0 + RP],
                                  bank[b][:, 0:512].rearrange("p (j t) -> p j t", t=128)[:, :, 0:RP],
                                  r=[PT[b]], w=[xT.t])

        def emit_tile(kind, s, t):
            T = 512 if kind == 'p' else 64
            NCH = T // 64
            NSB = (T + 127) // 128
            RP = min(128, T)
            par = (t % 2) if kind == 'p' else 0
            first = (t == 0) if kind == 'p' else False
            last = (t == NT - 1) if kind == 'p' else True
            has_hist = (not first)
            if kind == 'p':
                row0 = s * SEQ + t * 512
                xsrc = lambda sbi: xp[row0 + sbi * 128: row0 + sbi * 128 + RP, :]
                ydst = lambda sbi: yp[row0 + sbi * 128: row0 + sbi * 128 + RP, :]
                k_out = lambda sbi, c0: kp[s * 512 + sbi * 128: s * 512 + sbi * 128 + RP, c0:c0 + 512]
                v_out = lambda sbi, c0: vp[s * 512 + sbi * 128: s * 512 + sbi * 128 + RP, c0:c0 + 512]
                conv_out = convp[s * 3:(s + 1) * 3, :]
                C_out = Cp[s * 8:(s + 1) * 8]
                n_out = np_[s * 8:(s + 1) * 8, :]
                m_out = mp[s * 8:(s + 1) * 8, :]
            else:
                xsrc = lambda sbi: xs[0:64, :]
                ydst = lambda sbi: ys[0:64, :]
                k_out = lambda sbi, c0: ks[448:512, c0:c0 + 512]
                v_out = lambda sbi, c0: vs[448:512, c0:c0 + 512]
                conv_out = convs
                C_out, n_out, m_out = Cs, ns, ms
            kcur = par * 512
            kprev = (1 - par) * 512

            def run_a0(_):
                if first:
                    S.op('dve', lambda e: e.memset(hist.a, 0.0), w=[hist.t])
                    S.op('dve', lambda e: e.memset(Caug.a, 0.0), w=[Caug.t])
                    S.op('dve', lambda e: e.memset(mprev.a, 0.0), w=[mprev.t])
                if kind == 's':
                    for j_ in range(3):
                        S.dma('sp', hist.a[:, j_, :], sconv[j_:j_ + 1, :].rearrange("o (fc p) -> p (o fc)", p=128),
                              w=[hist.t], allow_slow_non_contiguous=True)
                    S.dma('sp', Caug.a[:, :, 0:128], sC.rearrange("h d e -> d h e"), w=[Caug.t])
                    S.dma('sp', Caug.a[:, :, 128:129], sn.rearrange("h (d o) -> d h o", o=1), w=[Caug.t],
                          allow_slow_non_contiguous=True)
                    S.dma('sp', mprev.a, sm, w=[mprev.t])
                    S.dma('pool', ks[0:448, :], ck[64:512, :])
                    S.dma('pool', vs[0:448, :], cv[64:512, :])
                    for pb in range(4):
                        S.dma('sp', xs_st.a[:, 0:1024], cv[pb * 128:(pb + 1) * 128, :], w=[xs_st.t])
                        S.op('dve', lambda e: e.tensor_copy(vA.a[:, (1 - par) * 4 + pb, :, :],
                                                             xs_st.a[:, 0:1024].rearrange("p (h d) -> p h d", d=64)),
                             r=[xs_st.t], w=[vA.t])
                        S.dma('sp', xs_st.a[:, 1024:2048], ck[pb * 128:(pb + 1) * 128, :], w=[xs_st.t])
                        for q in range(2):
                            b = nb()
                            for j in range(4):
                                hp = q * 4 + j
                                S.op('pe', lambda e: e.transpose(bank[b][:, j * 128:(j + 1) * 128],
                                                                  xs_st.a[:, 1024 + hp * 128:1024 + (hp + 1) * 128],
                                                                  ident_f.a),
                                     r=[xs_st.t, ident_f.t], w=[PT[b]], inc=(j == 3))
                            evac_copy(kT.a[:, q * 4:(q + 1) * 4, kprev + pb * 128:kprev + (pb + 1) * 128],
                                      bank[b][:, 0:512].rearrange("p (j t) -> p j t", t=128), r=[PT[b]], w=[kT.t])
                for sbi in range(NSB):
                    for q in range(4):
                        xq = kst[q % 2]
                        S.dma('sp', xq.a[0:RP, :], xsrc(sbi)[:, q * 512:(q + 1) * 512], w=[xq.t])
                        b = nb()
                        for j in range(4):
                            S.op('pe', lambda e: e.transpose(bank[b][:, j * 128:j * 128 + RP],
                                                              xq.a[0:RP, j * 128:(j + 1) * 128],
                                                              ident_f.a[0:RP, 0:RP]),
                                 r=[xq.t, ident_f.t], w=[PT[b]], inc=(j == 3))
                        evac_copy(xT.a[:, q * 4:(q + 1) * 4, sbi * 128:sbi * 128 + RP],
                                  bank[b][:, 0:512].rearrange("p (j t) -> p j t", t=128)[:, :, 0:RP],
                                  r=[PT[b]], w=[xT.t])
            add(None, run_a0)

            def f_group(g, evac):
                def run(bi):
                    wb = wbuf[bi]
                    for cc in range(4):
                        b = nb()
                        accum(b, 128, T, 16, lambda kc: wb.a[:, kc, cc * 128:(cc + 1) * 128],
                              lambda kc: xT.a[:, kc, 0:T], r=[wb.t, xT.t])
                        evac(g * 4 + cc, b)
                add([wload(wbi, 0, 16, g * 512, 512, t_wbiA if g < 6 else t_wbiB)], run)

            def t_group(g, evac):
                def run(bi):
                    wb = wbuf[bi]
                    for sbi in range(NSB):
                        b = nb()
                        accum(b, RP, 512, 16, lambda kc: xT.a[:, kc, sbi * 128:sbi * 128 + RP],
                              lambda kc: wb.a[:, kc, 0:512], r=[wb.t, xT.t])
                        evac(g, sbi, b)
                add([wload(wbi, 0, 16, g * 512, 512, t_wbiA if g < 6 else t_wbiB)], run)

            def evac_qk(chunk, b):
                if chunk < 8:
                    evac_copy(qT.a[:, chunk, 0:T], bank[b][:, 0:T], r=[PT[b]], w=[qT.t])
                else:
                    evac_copy(kT.a[:, chunk - 8, kcur:kcur + T], bank[b][:, 0:T], r=[PT[b]], w=[kT.t])
            for g in (0, 1, 2, 3):
                f_group(g, evac_qk)

            ost = [0]

            def evac_v(g, sbi, b):
                blk = par * 4 + sbi
                S.op('dve', lambda e: e.tensor_copy(vA.a[0:RP, blk, (g - 4) * 8:(g - 4) * 8 + 8, :],
                                                     bank[b][0:RP, 0:512].rearrange("p (h d) -> p h d", d=64)),
                     r=[PT[b]], w=[vA.t])
                if last:
                    ost[0] ^= 1
                    o = kst[ost[0]]
                    S.op('act', lambda e: e.copy(o.a[0:RP, :], bank[b][0:RP, 0:512]), r=[PT[b]], w=[o.t])
                    S.dma('pool', v_out(sbi, (g - 4) * 512), o.a[0:RP, :], r=[o.t])
            for g in (4, 5):
                t_group(g, evac_v)

            if last:
                def evac_kf(g, sbi, b):
                    ost[0] ^= 1
                    o = kst[ost[0]]
                    S.op('act', lambda e: e.copy(o.a[0:RP, :], bank[b][0:RP, 0:512]), r=[PT[b]], w=[o.t])
                    S.dma('pool', k_out(sbi, (g - 2) * 512), o.a[0:RP, :], r=[o.t])
                for g in (2, 3):
                    t_group(g, evac_kf)

            def run_attn(_):
                SK = 3
                STB = (0, 1, 2)
                seq = []
                info = {}
                for h in range(16):
                    hp, po = h // 2, (h % 2) * 64
                    blocks = []
                    if has_hist:
                        for pb in range(4):
                            N = min(T, 128 * (pb + 1))
                            corner = (0, 64, N - 64, N) if 128 * (pb + 1) <= T else None
                            blocks.append((kT.a[po:po + 64, hp, kprev + pb * 128:kprev + (pb + 1) * 128],
                                           vA.a[:, (1 - par) * 4 + pb, h, :], 128, 0, N, 512 - 128 * pb, corner))
                    for cb in range(NSB):
                        nk = min(128, T - cb * 128)
                        q0 = cb * 128
                        N = T - q0
                        corner = (64, 128, 0, 64) if nk == 128 else None
                        blocks.append((kT.a[po:po + 64, hp, kcur + q0:kcur + q0 + nk],
                                       vA.a[0:nk, par * 4 + cb, h, :], nk, q0, N, 0, corner))
                    info[h] = blocks
                    for i in range(len(blocks)):
                        seq.append((h, i))

                def tz_load(h):
                    S.dma('sp', tzb2[h % 2].a, tzd[h], r=[t_tzd], w=[tzb2[h % 2].t])

                def e_st(idx):
                    h, i = seq[idx]
                    if i == 0 and h + 1 < 16:
                        tz_load(h + 1)
                    hp, po = h // 2, (h % 2) * 64
                    kap, vap, nk, q0, N, toff, corner = info[h][i]
                    b = STB[idx % 3]
                    S.op('pe', lambda e: e.matmul(bank[b][0:nk, 0:N], kap, qT.a[po:po + 64, hp, q0:q0 + N],
                                                   start=True, stop=True), r=[kT.t, qT.t], w=[PT[b]])
                    tmp = sbt[idx % 3]
                    tz = tzb2[h % 2]
                    S.op('dve', lambda e: e.scalar_tensor_tensor(tmp.a[0:nk, 0:N], bank[b][0:nk, 0:N], 0.125,
                                                                  tz.a[0:nk, toff:toff + N], ALU.mult, ALU.add),
                         r=[PT[b], tz.t], w=[tmp.t])
                    P = pT[idx % 5]
                    S.op('act', lambda e: e.activation(P.a[0:nk, 0:N], tmp.a[0:nk, 0:N], AF.Exp),
                         r=[tmp.t], w=[P.t])
                    if corner is not None:
                        p0, p1, c0, c1 = corner
                        S.op('pool', lambda e: e.memset(P.a[p0:p1, c0:c1], 0.0), r=[P.t], w=[P.t])

                def e_pv(idx):
                    h, i = seq[idx]
                    hp, po = h // 2, (h % 2) * 64
                    kap, vap, nk, q0, N, toff, corner = info[h][i]
                    ob_, sb_ = (3, 4) if h % 2 == 0 else (5, 6)
                    P = pT[idx % 5]
                    fst, lst = (i == 0), (i == len(info[h]) - 1)
                    S.op('pe', lambda e: e.matmul(bank[ob_][po:po + 64, q0:q0 + N], vap[0:nk, :], P.a[0:nk, 0:N],
                                                   start=fst, stop=lst, skip_group_check=True),
                         r=[vA.t, P.t], w=[PT[ob_]])
                    S.op('pe', lambda e: e.matmul(bank[sb_][po:po + 64, q0:q0 + N], ones_bf.a[0:nk, :], P.a[0:nk, 0:N],
                                                   start=fst, stop=lst, skip_group_check=True),
                         r=[ones_bf.t, P.t], w=[PT[sb_]])
                    if lst:
                        tails.append(tail_stages(h))

                def tail_stages(h):
                    hp, po = h // 2, (h % 2) * 64
                    ob_, sb_ = (3, 4) if h % 2 == 0 else (5, 6)
                    sl = slice(po, po + 64)
                    Q, rs, tt = osq[h % 2], att_r[h % 2], att_t[h % 2]
                    ab = 7

                    def t1():
                        S.op('act', lambda e: e.activation(Q.a[sl, 0:T], bank[ob_][sl, 0:T], AF.Square), r=[PT[ob_]], w=[Q.t])
                        S.op('act', lambda e: e.activation(rs.a[sl, 0:T], bank[sb_][sl, 0:T], AF.Square), r=[PT[sb_]], w=[rs.t])

                    def t2():
                        S.op('pe', lambda e: e.matmul(bank[ab][sl, 0:T], c64.a[sl, 0:64], Q.a[sl, 0:T], start=True, stop=True),
                             r=[c64.t, Q.t], w=[PT[ab]])

                    def t3():
                        S.op('dve', lambda e: e.scalar_tensor_tensor(tt.a[sl, 0:T], rs.a[sl, 0:T], 1e-6, bank[ab][sl, 0:T],
                                                                      ALU.mult, ALU.add), r=[PT[ab], rs.t], w=[tt.t])

                    def t4():
                        S.op('act', lambda e: e.activation(tt.a[sl, 0:T], tt.a[sl, 0:T], AF.Ln), r=[tt.t], w=[tt.t])
                        S.op('act', lambda e: e.activation(tt.a[sl, 0:T], tt.a[sl, 0:T], AF.Exp, scale=-0.5),
                             r=[tt.t], w=[tt.t])

                    def t5():
                        S.op('dve', lambda e: e.scalar_tensor_tensor(concatT.a[sl, hp, 0:T], bank[ob_][sl, 0:T],
                                                                      gAT.a[sl, hp:hp + 1], tt.a[sl, 0:T], ALU.mult, ALU.mult),
                             r=[PT[ob_], gAT.t, tt.t], w=[concatT.t])
                    return [t1, t2, t3, t4, t5]

                tails = []
                tz_load(0)
                n = len(seq)
                for idx in range(n + SK):
                    if idx < n:
                        e_st(idx)
                    if idx - SK >= 0:
                        e_pv(idx - SK)
                    for tl in list(tails):
                        tl.pop(0)()
                        if not tl:
                            tails.remove(tl)
                while tails:
                    for tl in list(tails):
                        tl.pop(0)()
                        if not tl:
                            tails.remove(tl)
                S.barrier()
            add(None, run_attn)

            def evac_raw(chunk, b):
                if chunk < 48:
                    fc = chunk - 24
                    rb_ = raw[fc % 2]
                    S.op('dve', lambda e: e.tensor_copy(rb_.a[:, 0:3], hist.a[:, :, fc]), r=[hist.t], w=[rb_.t])
                    S.op('act', lambda e: e.copy(rb_.a[:, 3:3 + T], bank[b][:, 0:T]), r=[PT[b]], w=[rb_.t])
                    S.op('dve', lambda e: e.tensor_copy(hist.a[:, :, fc], rb_.a[:, T:T + 3]), r=[rb_.t], w=[hist.t])
                    S.op('dve', lambda e: e.tensor_scalar(acc.a[:, 0:T], rb_.a[:, 0:T], wconvT.a[:, 0, fc:fc + 1],
                                                           bconvT.a[:, fc:fc + 1], ALU.mult, ALU.add),
                         r=[rb_.t, wconvT.t, bconvT.t], w=[acc.t])
                    for j in (1, 2, 3):
                        S.op('dve', lambda e: e.scalar_tensor_tensor(acc.a[:, 0:T], rb_.a[:, j:j + T],
                                                                      wconvT.a[:, j, fc:fc + 1], acc.a[:, 0:T],
                                                                      ALU.mult, ALU.add), r=[rb_.t, acc.t], w=[acc.t])
                    S.op('act', lambda e: e.activation(sigt.a[:, 0:T], acc.a[:, 0:T], AF.Sigmoid), r=[acc.t], w=[sigt.t])
                    dst = qmT.a[:, fc, 0:T] if fc < 8 else kmT.a[:, fc - 8, 0:T]
                    dt_ = qmT.t if fc < 8 else kmT.t
                    sc = 1.0 if fc < 8 else KSCALE
                    S.op('dve', lambda e: e.scalar_tensor_tensor(dst, acc.a[:, 0:T], sc, sigt.a[:, 0:T], ALU.mult, ALU.mult),
                         r=[acc.t, sigt.t], w=[dt_])
                else:
                    hc = chunk - 48
                    S.op('act', lambda e: e.activation(gsT.a[:, hc, 0:T], bank[b][:, 0:T], AF.Sigmoid), r=[PT[b]], w=[gsT.t])
                    S.op('dve', lambda e: e.tensor_scalar(gsT.a[:, hc, 0:T], gsT.a[:, hc, 0:T], gBT.a[:, hc:hc + 1], None,
                                                           ALU.mult), r=[gsT.t, gBT.t], w=[gsT.t])
            for g in (6, 7, 8, 9, 12, 13):
                f_group(g, evac_raw)

            def evac_vb(g, sbi, b):
                S.op('act', lambda e: e.copy(vB.a[0:RP, sbi, (g - 10) * 4:(g - 10) * 4 + 4, 0:128],
                                             bank[b][0:RP, 0:512].rearrange("p (h d) -> p h d", d=128)),
                     r=[PT[b]], w=[vB.t])
            for g in (10, 11):
                t_group(g, evac_vb)

            def run_mlstm(_):
                S.op('dve', lambda e: e.memset(vB.a[:, :, :, 128:129], 1.0), r=[vB.t], w=[vB.t])
                if last:
                    for j_ in range(3):
                        S.dma('pool', conv_out[j_:j_ + 1, :].rearrange("o (fc p) -> p (o fc)", p=128), hist.a[:, j_, :],
                              r=[hist.t], allow_slow_non_contiguous=True)
                bi_, bf_ = nb(), nb()
                accum(bi_, 8, T, 16, lambda kc: wtail.a[:, kc, 0:8], lambda kc: xT.a[:, kc, 0:T], r=[wtail.t, xT.t])
                accum(bf_, 8, T, 16, lambda kc: wtail.a[:, kc, 8:16], lambda kc: xT.a[:, kc, 0:T], r=[wtail.t, xT.t])
                gi, ga, Fn, u = G[0], G[1], G[2], G[3]
                S.op('dve', lambda e: e.tensor_scalar(gi.a[:, 0:T], bank[bi_][0:8, 0:T], big.a[:, 0:1], None, ALU.add),
                     r=[PT[bi_], big.t], w=[gi.t])
                S.op('act', lambda e: e.activation(ga.a[:, 0:T], bank[bf_][0:8, 0:T], AF.Exp, bias=nbf.a[:, 0:1], scale=-1.0),
                     r=[PT[bf_], nbf.t], w=[ga.t])
                S.op('act', lambda e: e.activation(ga.a[:, 0:T], ga.a[:, 0:T], AF.Ln, bias=1.0, scale=1.0),
                     r=[ga.t], w=[ga.t])
                S.op('dve', lambda e: e.tensor_tensor_scan(Fn.a[:, 0:T], rm.a[:, 0:T], ga.a[:, 0:T], 0.0, ALU.mult, ALU.add),
                     r=[rm.t, ga.t], w=[Fn.t])
                S.op('dve', lambda e: e.tensor_tensor(u.a[:, 0:T], gi.a[:, 0:T], Fn.a[:, 0:T], ALU.add),
                     r=[gi.t, Fn.t], w=[u.t])
                cm = ga
                S.op('dve', lambda e: e.tensor_tensor_scan(cm.a[:, 0:T], nm.a[:, 0:T], u.a[:, 0:T], 0.0, ALU.add, ALU.max),
                     r=[nm.t, u.t], w=[cm.t])
                cmL = gsm.a[:, 0:NCH]
                FL = gsm.a[:, 8:8 + NCH]
                mall = gsm.a[:, 16:17 + NCH]
                ex = gsm.a[:, 32:32 + 3 * NCH]
                S.op('dve', lambda e: e.tensor_copy(cmL, cm.a[:, 0:T].rearrange("p (c t) -> p c t", t=64)[:, :, 63]),
                     r=[cm.t], w=[gsm.t])
                S.op('dve', lambda e: e.tensor_scalar(FL, Fn.a[:, 0:T].rearrange("p (c t) -> p c t", t=64)[:, :, 63],
                                                       -1.0, None, ALU.mult), r=[Fn.t], w=[gsm.t])
                S.op('dve', lambda e: e.tensor_copy(mall[:, 0:1], mprev.a), r=[mprev.t], w=[gsm.t])
                S.op('dve', lambda e: e.tensor_tensor_scan(mall[:, 1:NCH + 1], cmL, FL, mprev.a[:, 0:1], ALU.max, ALU.add),
                     r=[gsm.t, mprev.t], w=[gsm.t])
                S.op('dve', lambda e: e.tensor_copy(ex[:, 0:NCH], mall[:, 0:NCH]), r=[gsm.t], w=[gsm.t])
                S.op('dve', lambda e: e.tensor_tensor(ex[:, NCH:2 * NCH], FL, mall[:, 1:NCH + 1], ALU.subtract),
                     r=[gsm.t], w=[gsm.t])
                S.op('dve', lambda e: e.tensor_tensor(ex[:, 2 * NCH:3 * NCH], ex[:, NCH:2 * NCH], mall[:, 0:NCH], ALU.add),
                     r=[gsm.t], w=[gsm.t])
                S.op('dve', lambda e: e.tensor_copy(mprev.a, mall[:, NCH:NCH + 1]), r=[gsm.t], w=[mprev.t])
                if last:
                    S.dma('pool', m_out, mprev.a, r=[mprev.t])
                for sbi in range(NSB):
                    b = nb()
                    S.op('pe', lambda e: e.transpose(bank[b][0:RP, 0:8], u.a[:, sbi * 128:sbi * 128 + RP], ident_f.a[0:8, 0:8]),
                         r=[u.t, ident_f.t], w=[PT[b]])
                    S.op('act', lambda e: e.activation(eu_tok.a[0:RP, sbi, :], bank[b][0:RP, 0:8], AF.Exp),
                         r=[PT[b]], w=[eu_tok.t])
                    b = nb()
                    S.op('pe', lambda e: e.transpose(bank[b][0:RP, 0:8], Fn.a[:, sbi * 128:sbi * 128 + RP], ident_f.a[0:8, 0:8]),
                         r=[Fn.t, ident_f.t], w=[PT[b]])
                    S.op('act', lambda e: e.activation(enf_tok.a[0:RP, sbi, :], bank[b][0:RP, 0:8], AF.Exp),
                         r=[PT[b]], w=[enf_tok.t])
                CaT = [Tok() for _ in range(8)]
                CsT = [Tok() for _ in range(8)]
                for h in range(8):
                    b = nb()
                    S.op('pe', lambda e: e.matmul(bank[b][:, 0:3 * NCH], selall.a[:, h * 128:(h + 1) * 128], ex,
                                                   start=True, stop=True), r=[selall.t, gsm.t], w=[PT[b]])
                    S.op('act', lambda e: e.activation(bc.a[:, h, 0:3 * NCH], bank[b][:, 0:3 * NCH], AF.Exp),
                         r=[PT[b]], w=[bc.t])
                    S.op('act', lambda e: e.activation(Csbf.a[:, h, :], Caug.a[:, h, :], AF.Copy, scale=bc.a[:, h, 0:1]),
                         r=[Caug.t, bc.t], w=[Csbf.t, CsT[h]])
                for sbi in range(NSB):
                    b = nb()
                    bb = bank[b].bitcast(BF16)
                    for h in range(8):
                        S.op('pe', lambda e: e.transpose(bb[0:RP, h * 128:(h + 1) * 128],
                                                          kmT.a[:, h, sbi * 128:sbi * 128 + RP], ident_b.a),
                             r=[kmT.t, ident_b.t], w=[PT[b]], inc=(h == 7))
                    evac_copy(kTok.a[0:RP, sbi, :, :], bb[0:RP, :].rearrange("p (h d) -> p h d", d=128), r=[PT[b]], w=[kTok.t])
                steps = [(c, h) for c in range(NCH) for h in range(8)]
                NB_ = 6

                def geo(i):
                    c, h = steps[i]
                    sbi, pbase = c // 2, (c % 2) * 64
                    return c, h, sbi, pbase, slice(pbase, pbase + 64), slice(c * 64, c * 64 + 64)

                def s0(i):
                    c, h, sbi, pbase, sl, cols = geo(i)
                    b1 = (0, 1)[i % 2]
                    S.op('pe', lambda e: e.matmul(bank[b1][sl, 0:64], kmT.a[:, h, cols], qmT.a[:, h, cols],
                                                   start=True, stop=True), r=[kmT.t, qmT.t], w=[PT[b1]])

                def s1(i):
                    c, h, sbi, pbase, sl, cols = geo(i)
                    b1 = (0, 1)[i % 2]
                    P, vv = Pm[i % NB_], vpm[i % NB_]
                    S.op('dve', lambda e: e.tensor_tensor(P.a[sl, :], bank[b1][sl, 0:64], mask64.a[sl, :], ALU.mult),
                         r=[PT[b1], mask64.t], w=[P.t])
                    S.op('pool' if i % 2 else 'dve',
                         lambda e: e.tensor_scalar(vv.a[sl, :], vB.a[sl, sbi, h, :], eu_tok.a[sl, sbi, h:h + 1],
                                                   None, ALU.mult), r=[vB.t, eu_tok.t], w=[vv.t])

                def s2(i):
                    c, h, sbi, pbase, sl, cols = geo(i)
                    P, vv = Pm[i % NB_], vpm[i % NB_]
                    b2 = (2, 3)[i % 2]
                    b3 = (4, 5)[i % 2]
                    S.op('pe', lambda e: e.matmul(bank[b2][sl, 0:129], P.a[sl, :], vv.a[sl, :], start=True, stop=False),
                         r=[P.t, vv.t], w=[PT[b2]], inc=False)
                    S.op('pe', lambda e: e.matmul(bank[b2][sl, 0:129], qmT.a[:, h, cols], Csbf.a[:, h, :],
                                                   start=False, stop=True), r=[qmT.t, CsT[h]], w=[PT[b2]])
                    S.op('pe', lambda e: e.matmul(bank[b3][:, 0:129], kTok.a[sl, sbi, h, :], vv.a[sl, :],
                                                   start=True, stop=True), r=[kTok.t, vv.t], w=[PT[b3]])

                def s3(i):
                    c, h, sbi, pbase, sl, cols = geo(i)
                    b2 = (2, 3)[i % 2]
                    b3 = (4, 5)[i % 2]
                    ut, hh, sm_ = updt[i % NB_], hn[i % NB_], sml[i % NB_]
                    S.op('act', lambda e: e.activation(ut.a, bank[b3][:, 0:129], AF.Copy,
                                                       scale=bc.a[:, h, NCH + c:NCH + c + 1]),
                         r=[PT[b3], bc.t], w=[ut.t])
                    dd, rec = sm_.a[sl, 0:1], sm_.a[sl, 1:2]
                    S.op('dve', lambda e: e.tensor_scalar(rec, bank[b2][sl, 128:129], -1.0, enf_tok.a[sl, sbi, h:h + 1],
                                                           ALU.mult, ALU.max), r=[PT[b2], enf_tok.t], w=[sm_.t])
                    S.op('dve', lambda e: e.tensor_tensor(dd, rec, bank[b2][sl, 128:129], ALU.max),
                         r=[PT[b2], sm_.t], w=[sm_.t])
                    S.op('dve', lambda e: e.reciprocal(rec, dd), r=[sm_.t], w=[sm_.t])
                    S.op('dve', lambda e: e.tensor_scalar(hh.a[sl, :], bank[b2][sl, 0:128], rec, None, ALU.mult),
                         r=[PT[b2], sm_.t], w=[hh.t])

                def s4(i):
                    c, h, sbi, pbase, sl, cols = geo(i)
                    ut, hh, sm_ = updt[i % NB_], hn[i % NB_], sml[i % NB_]
                    ss = sm_.a[sl, 2:3]
                    S.op('dve', lambda e: e.scalar_tensor_tensor(Caug.a[:, h, :], Caug.a[:, h, :],
                                                                  bc.a[:, h, 2 * NCH + c:2 * NCH + c + 1], ut.a,
                                                                  ALU.mult, ALU.add), r=[CaT[h], bc.t, ut.t], w=[CaT[h]])
                    if c + 1 < NCH:
                        S.op('act', lambda e: e.activation(Csbf.a[:, h, :], Caug.a[:, h, :], AF.Copy,
                                                           scale=bc.a[:, h, c + 1:c + 2]),
                             r=[CaT[h], bc.t], w=[CsT[h]])
                    S.op('act', lambda e: e.activation(junk.a[sl, 0:128], hh.a[sl, :], AF.Square, accum_out=ss),
                         r=[hh.t], w=[junk.t, sm_.t])
                    S.op('act', lambda e: e.activation(ss, ss, AF.Ln, bias=eps6.a[sl, 0:1], scale=1.0 / 128),
                         r=[sm_.t, eps6.t], w=[sm_.t])
                    S.op('act', lambda e: e.activation(ss, ss, AF.Exp, scale=-0.5), r=[sm_.t], w=[sm_.t])
                    S.op('act', lambda e: e.activation(hh.a[sl, :], hh.a[sl, :], AF.Copy, scale=ss),
                         r=[hh.t, sm_.t], w=[hh.t])

                def s6(i):
                    c, h, sbi, pbase, sl, cols = geo(i)
                    hh = hn[i % NB_]
                    b4 = (6, 7)[i % 2]
                    S.op('pe', lambda e: e.transpose(bank[b4][:, 0:64], hh.a[sl, :], ident_f.a[sl, pbase:pbase + 64]),
                         r=[hh.t, ident_f.t], w=[PT[b4]])

                def s7(i):
                    c, h, sbi, pbase, sl, cols = geo(i)
                    b4 = (6, 7)[i % 2]
                    S.op('dve', lambda e: e.tensor_tensor(concatT.a[:, 8 + h, cols], bank[b4][:, 0:64], gsT.a[:, h, cols],
                                                           ALU.mult), r=[PT[b4], gsT.t], w=[concatT.t])

                stages = [s0, s1, s2, s3, s4, s6, s7]
                ns_ = len(steps)
                for it_ in range(ns_ + len(stages) - 1):
                    for k_, fn in enumerate(stages):
                        i = it_ - k_
                        if 0 <= i < ns_:
                            fn(i)
                S.op('dve', lambda e: e.tensor_copy(sml[0].a[0:8, 0:1], sml[0].a[0:8, 1:2]), r=CaT + CsT, w=[Caug.t, Csbf.t])
                if last:
                    S.dma('pool', C_out.rearrange("h d e -> d h e"), Caug.a[:, :, 0:128], r=[Caug.t])
                    S.dma('pool', n_out.rearrange("h (d o) -> d h o", o=1), Caug.a[:, :, 128:129], r=[Caug.t],
                          allow_slow_non_contiguous=True)
                S.barrier()
            add(None, run_mlstm)

            for cg in range(4):
                def run(bi, cg=cg):
                    wb = wbuf[bi]
                    for sbi in range(NSB):
                        b = nb()
                        accum(b, RP, 512, 16, lambda kc: concatT.a[:, kc, sbi * 128:sbi * 128 + RP],
                              lambda kc: wb.a[:, kc, 0:512], r=[wb.t, concatT.t])
                        ost[0] ^= 1
                        o = ostage[ost[0]]
                        evac_copy(o.a[0:RP, :], bank[b][0:RP, 0:512], r=[PT[b]], w=[o.t])
                        S.dma('pool', mixd[sbi * 128:sbi * 128 + RP, cg * 512:(cg + 1) * 512], o.a[0:RP, :], r=[o.t], w=[t_mixd])
                add([wload(wbo, 0, 16, cg * 512, 512, t_wbo)], run)

            def run_ln1(_):
                ln_pass(T, mixd, t_mixd, xsrc, Tok(), ln1_g, ln1_b, lambda sbi: x1d[sbi * 128:sbi * 128 + RP, :], t_x1d, True)
            add(None, run_ln1)

            for k in range(11):
                def run_g(bi, k=k):
                    wb = wbuf[bi]
                    for cc in range(4):
                        bA = nb()
                        accum(bA, 128, T, 16, lambda kc: wb.a[:, kc, cc * 128:(cc + 1) * 128],
                              lambda kc: xT.a[:, kc, 0:T], r=[wb.t, xT.t])
                        S.op('act', lambda e: e.activation(sg4[cc].a[:, 0:T], bank[bA][:, 0:T], AF.Silu),
                             r=[PT[bA]], w=[sg4[cc].t])
                add([wload(wbg, 0, 16, k * 512, 512, t_wbg)], run_g)

                def run_u(bi, k=k):
                    wb = wbuf[bi]
                    for cc in range(4):
                        bB = nb()
                        accum(bB, 128, T, 16, lambda kc: wb.a[:, kc, cc * 128:(cc + 1) * 128],
                              lambda kc: xT.a[:, kc, 0:T], r=[wb.t, xT.t])
                        S.op('dve', lambda e: e.tensor_tensor(hT.a[:, k * 4 + cc, 0:T], sg4[cc].a[:, 0:T], bank[bB][:, 0:T],
                                                               ALU.mult), r=[sg4[cc].t, PT[bB]], w=[hT.t])
                add([wload(wbu, 0, 16, k * 512, 512, t_wbu)], run_u)

            for cg in range(4):
                for piece, (k0, nk) in enumerate(((0, 16), (16, 16), (32, 12))):
                    def run(bi, cg=cg, piece=piece, k0=k0, nk=nk):
                        wb = wbuf[bi]
                        for sbi in range(NSB):
                            b = sbi
                            accum(b, RP, 512, nk, lambda kc: hT.a[:, k0 + kc, sbi * 128:sbi * 128 + RP],
                                  lambda kc: wb.a[:, kc, 0:512], r=[wb.t, hT.t], start=(piece == 0), stop=(piece == 2))
                            if piece == 2:
                                ost[0] ^= 1
                                o = ostage[ost[0]]
                                evac_copy(o.a[0:RP, :], bank[b][0:RP, 0:512], r=[PT[b]], w=[o.t])
                                S.dma('pool', ypd[sbi * 128:sbi * 128 + RP, cg * 512:(cg + 1) * 512], o.a[0:RP, :],
                                      r=[o.t], w=[t_ypd])
                    add([wload(wbd, k0, nk, cg * 512, 512, t_wbd)], run)

            def run_ln2(_):
                ln_pass(T, ypd, t_ypd, lambda sbi: x1d[sbi * 128:sbi * 128 + RP, :], t_x1d, ln2_g, ln2_b, ydst, Tok(), False)
                S.barrier_pe()
            add(None, run_ln2)

        t_mixd, t_ypd, t_x1d = Tok(), Tok(), Tok()
        for s in range(NSEQ):
            for t in range(NT):
                emit_tile('p', s, t)
        if SAMPLE:
            emit_tile('s', 0, 0)

        widx = [i for i, it in enumerate(items) if it[0] is not None]
        ptr = [0]
        bufof = {}

        def issue_next():
            if ptr[0] < len(widx):
                i = widx[ptr[0]]
                bi = ptr[0] % 2
                bufof[i] = bi
                for (dram, k0, nk, c0, ncols, toks, dc0) in items[i][0]:
                    S.dma('sp', wbuf[bi].a[:, 0:nk, dc0:dc0 + ncols],
                          dram[k0 * 128:(k0 + nk) * 128, c0:c0 + ncols].rearrange("(kc p) n -> p kc n", p=128),
                          r=toks, w=[wbuf[bi].t])
                ptr[0] += 1
        issue_next()
        for i, (loads, run) in enumerate(items):
            if loads is not None:
                issue_next()
                run(bufof[i])
            else:
                run(None)
        S.finish()
    return nc


_NC_CACHE = {}


def _prep_common(inp):
    f = lambda a: np.ascontiguousarray(a, dtype=np.float32)
    return {
        "w_in": f(inp["w_in"][0]), "b_ig": f(inp["b_igate"][0].reshape(8, 1)), "b_fg": f(inp["b_fgate"][0].reshape(8, 1)),
        "w_conv": f(inp["w_conv"][0]), "b_conv": f(inp["b_conv"][0].reshape(1, 2048)),
        "rel_bias": f(inp["rel_bias"][0]), "g_attn": f(inp["g_attn_norm"][0].reshape(1, 1024)),
        "g_mlstm": f(inp["g_mlstm_norm"][0].reshape(1, 1024)), "w_out": f(inp["w_out"][0]),
        "ln1_g": f(inp["ln1_g"][0].reshape(1, D)), "ln1_b": f(inp["ln1_b"][0].reshape(1, D)),
        "w_g": f(inp["w_ffn_gate"][0]), "w_u": f(inp["w_ffn_up"][0]), "w_d": f(inp["w_ffn_down"][0]),
        "ln2_g": f(inp["ln2_g"][0].reshape(1, D)), "ln2_b": f(inp["ln2_b"][0].reshape(1, D)),
    }


def kernel(**inp):
    n = 8
    if "nc" not in _NC_CACHE:
        _NC_CACHE["nc"] = build()
    nc = _NC_CACHE["nc"]
    common = _prep_common(inp)
    f = lambda a: np.ascontiguousarray(a, dtype=np.float32)
    in_maps = []
    for c in range(n):
        m = dict(common)
        m["xp"] = f(inp["x_prompt"][2 * c:2 * c + 2].reshape(2 * 2048, D))
        m["xs"] = f(inp["x_sample"][c])
        m["ck"] = f(inp["cache_attn_k"][0, c].reshape(512, 1024))
        m["cv"] = f(inp["cache_attn_v"][0, c].reshape(512, 1024))
        m["sconv"] = f(inp["state_conv"][0, c])
        m["sC"] = f(inp["state_mlstm_C"][0, c])
        m["sn"] = f(inp["state_mlstm_n"][0, c])
        m["sm"] = f(inp["state_mlstm_m"][0, c].reshape(8, 1))
        in_maps.append(m)
    res = run_bass_kernel_spmd(nc, in_maps, core_ids=list(range(n)))
    R = res.results
    cat = lambda k: np.concatenate([r[k] for r in R], axis=0)
    y_p = cat("yp").reshape(16, 2048, D)
    y_s = np.stack([r["ys"] for r in R]).reshape(8, 64, D)
    k_p = cat("kp").reshape(1, 16, 512, 16, 64)
    v_p = cat("vp").reshape(1, 16, 512, 16, 64)
    conv_p = cat("convp").reshape(1, 16, 3, 2048)
    C_p = cat("Cp").reshape(1, 16, 8, 128, 128)
    n_p = cat("np").reshape(1, 16, 8, 128)
    m_p = cat("mp").reshape(1, 16, 8)
    k_s = np.stack([r["ks"] for r in R]).reshape(1, 8, 512, 16, 64)
    v_s = np.stack([r["vs"] for r in R]).reshape(1, 8, 512, 16, 64)
    conv_s = np.stack([r["convs"] for r in R]).reshape(1, 8, 3, 2048)
    C_s = np.stack([r["Cs"] for r in R]).reshape(1, 8, 8, 128, 128)
    n_s = np.stack([r["ns"] for r in R]).reshape(1, 8, 8, 128)
    m_s = np.stack([r["ms"] for r in R]).reshape(1, 8, 8)
    outs = (y_p, y_s, k_p, v_p, conv_p, C_p, n_p, m_p, k_s, v_s, conv_s, C_s, n_s, m_s)
    return tuple(np.ascontiguousarray(o, dtype=np.float32) for o in outs)
```

```python
import numpy as np
from contextlib import ExitStack
import concourse.bass as bass
import concourse.mybir as mybir
from concourse.bass_utils import run_bass_kernel_spmd

F32 = mybir.dt.float32
BF16 = mybir.dt.bfloat16
AF = mybir.ActivationFunctionType
ALU = mybir.AluOpType

D = 2048
DIN = 7184
DFF = 5632
ALPHA = 2.0 ** 0.25
KSCALE = 128.0 ** -0.5


class Tok:
    __slots__ = ("w", "r", "excl")

    def __init__(self, excl=False):
        self.w = None
        self.r = {}
        self.excl = excl


class Sched:
    NDMA = 40

    def __init__(self, nc, es):
        self.nc = nc
        self.es = es
        self.eng = {'pe': nc.tensor, 'act': nc.scalar, 'dve': nc.vector, 'pool': nc.gpsimd, 'sp': nc.sync}
        self.sem = {k: es.enter_context(nc.semaphore("s_" + k)) for k in ('pe', 'act', 'dve', 'pool')}
        self.cnt = {k: 0 for k in self.sem}
        self.seen = {k: {} for k in self.eng}
        self.dsem = [es.enter_context(nc.semaphore("d%d" % i)) for i in range(self.NDMA)]
        self.dtot = [0] * self.NDMA
        self.dnext = 0
        self.pending = {k: [] for k in self.eng}

    def sbuf(self, name, shape, dtype):
        return self.es.enter_context(self.nc.sbuf_tensor(name, shape, dtype))

    def psum(self, name, shape, dtype):
        return self.es.enter_context(self.nc.psum_tensor(name, shape, dtype))

    def _wait(self, eng, ev):
        key, sem, val = ev
        if self.seen[eng].get(key, 0) >= val:
            return
        self.eng[eng].wait_ge(sem, val)
        self.seen[eng][key] = val

    def _deps(self, eng, r, w):
        for t in r:
            if t.w is not None:
                if t.w[0] == eng and eng == 'pe':
                    continue
                self._wait(eng, t.w)
            if t.excl:
                for ev in list(t.r.values()):
                    if ev[0] != eng:
                        self._wait(eng, ev)
        for t in w:
            if t.w is not None and not (t.w[0] == eng and eng != 'pool'):
                self._wait(eng, t.w)
            for ev in t.r.values():
                if ev[0] == eng and eng != 'pool':
                    continue
                self._wait(eng, ev)

    def _register(self, ev, r, w):
        for t in r:
            old = t.r.get(ev[0])
            if old is None or old[2] < ev[2]:
                t.r[ev[0]] = ev
        for t in w:
            t.w = ev
            t.r = {}

    def op(self, eng, fn, r=(), w=(), inc=True):
        self._deps(eng, r, w)
        inst = fn(self.eng[eng])
        if not inc:
            self.pending[eng].append((tuple(r), tuple(w)))
            return None
        self.cnt[eng] += 1
        inst.then_inc(self.sem[eng], 1)
        ev = (eng, self.sem[eng], self.cnt[eng])
        for pr, pw in self.pending[eng]:
            self._register(ev, pr, pw)
        self.pending[eng] = []
        self._register(ev, r, w)
        return ev

    def dma(self, q, out, in_, r=(), w=(), **kw):
        lo, hi = (0, 24) if q == 'sp' else (24, self.NDMA)
        if not hasattr(self, 'qn'):
            self.qn = {}
        i = self.qn.get(q, lo)
        self.qn[q] = lo + (i + 1 - lo) % (hi - lo)
        self._deps(q, r, w)
        key = 'd%d' % i
        if self.dtot[i] > 0:
            self._wait(q, (key, self.dsem[i], self.dtot[i]))
        inst = self.eng[q].dma_start(out=out, in_=in_, **kw)
        self.dtot[i] += 16
        inst.then_inc(self.dsem[i], 16)
        ev = (key, self.dsem[i], self.dtot[i])
        self._register(ev, r, w)
        return ev

    def barrier(self):
        for e in ('pe', 'act', 'dve', 'pool', 'sp'):
            for e2 in ('pe', 'act', 'dve', 'pool'):
                if e2 != e and self.cnt[e2] > 0:
                    self._wait(e, (e2, self.sem[e2], self.cnt[e2]))
            for i in range(self.NDMA):
                if self.dtot[i] > 0:
                    self._wait(e, ('d%d' % i, self.dsem[i], self.dtot[i]))

    def barrier_pe(self):
        if self.cnt['pe'] > 0:
            for e in ('act', 'dve', 'pool', 'sp'):
                self._wait(e, ('pe', self.sem['pe'], self.cnt['pe']))

    def finish(self):
        self.barrier()


class Buf:
    def __init__(self, ap):
        self.a = ap
        self.t = Tok()


def build(NSEQ=2, SEQ=2048, SAMPLE=True, DBG=False):
    nc = bass.Bass("TRN2", target_bir_lowering=False)
    NT = SEQ // 512

    def din(name, shape):
        return nc.dram_tensor(name, shape, F32, kind="ExternalInput").ap()

    def dout(name, shape):
        return nc.dram_tensor(name, shape, F32, kind="ExternalOutput").ap()

    def dint(name, shape, dt):
        return nc.dram_tensor(name, shape, dt, kind="Internal").ap()

    xp = din("xp", [NSEQ * SEQ, D])
    xs = din("xs", [64, D])
    ck = din("ck", [512, 1024])
    cv = din("cv", [512, 1024])
    sconv = din("sconv", [3, 2048])
    sC = din("sC", [8, 128, 128])
    sn = din("sn", [8, 128])
    sm = din("sm", [8, 1])
    w_in = din("w_in", [D, DIN])
    b_ig = din("b_ig", [8, 1])
    b_fg = din("b_fg", [8, 1])
    w_conv = din("w_conv", [4, 2048])
    b_conv = din("b_conv", [1, 2048])
    rel_bias = din("rel_bias", [16, 513])
    g_attn = din("g_attn", [1, 1024])
    g_mlstm = din("g_mlstm", [1, 1024])
    w_out = din("w_out", [D, D])
    ln1_g = din("ln1_g", [1, D])
    ln1_b = din("ln1_b", [1, D])
    w_g = din("w_g", [D, DFF])
    w_u = din("w_u", [D, DFF])
    w_d = din("w_d", [DFF, D])
    ln2_g = din("ln2_g", [1, D])
    ln2_b = din("ln2_b", [1, D])

    yp = dout("yp", [NSEQ * SEQ, D])
    ys = dout("ys", [64, D])
    kp = dout("kp", [NSEQ * 512, 1024])
    vp = dout("vp", [NSEQ * 512, 1024])
    convp = dout("convp", [NSEQ * 3, 2048])
    Cp = dout("Cp", [NSEQ * 8, 128, 128])
    np_ = dout("np", [NSEQ * 8, 128])
    mp = dout("mp", [NSEQ * 8, 1])
    ks = dout("ks", [512, 1024])
    vs = dout("vs", [512, 1024])
    convs = dout("convs", [3, 2048])
    Cs = dout("Cs", [8, 128, 128])
    ns = dout("ns", [8, 128])
    ms = dout("ms", [8, 1])

    wbi = dint("wbi", [D, DIN], BF16)
    wbo = dint("wbo", [D, D], BF16)
    wbg = dint("wbg", [D, DFF], BF16)
    wbu = dint("wbu", [D, DFF], BF16)
    wbd = dint("wbd", [DFF, D], BF16)
    extd = dint("extd", [16, 768], F32)
    tzd = dint("tzd", [16, 128, 640], F32)
    mixd = dint("mixd", [512, D], F32)
    x1d = dint("x1d", [512, D], F32)
    ypd = dint("ypd", [512, D], F32)

    es = ExitStack()
    with es:
        S = Sched(nc, es)

        def sb(name, shape, dt=F32):
            return Buf(S.sbuf(name, shape, dt)[:])

        bank = [S.psum("pb%d" % i, [128, 512], F32)[:] for i in range(8)]
        PT = [Tok(excl=True) for _ in range(8)]
        bctr = [0]

        def nb():
            bctr[0] = (bctr[0] + 1) % 8
            return bctr[0]

        ident_f = sb("ident_f", [128, 128])
        ident_b = sb("ident_b", [128, 128], BF16)
        Jm = sb("Jm", [128, 128])
        mask64 = sb("mask64", [128, 64])
        selall = sb("selall", [8, 8 * 128])
        rm = sb("rm", [8, 512])
        nm = sb("nm", [8, 512])
        ones_bf = sb("ones_bf", [128, 64], BF16)
        S.op('pool', lambda e: e.memset(ident_f.a, 0.0), w=[ident_f.t])
        S.op('pool', lambda e: e.affine_select(out=ident_f.a, in_=ident_f.a, compare_op=ALU.not_equal, fill=1.0,
                                                base=0, pattern=[[-1, 128]], channel_multiplier=1),
             r=[ident_f.t], w=[ident_f.t])
        S.op('dve', lambda e: e.tensor_copy(ident_b.a, ident_f.a), r=[ident_f.t], w=[ident_b.t])
        S.op('pool', lambda e: e.memset(Jm.a, 0.0), w=[Jm.t])
        S.op('pool', lambda e: e.affine_select(out=Jm.a, in_=Jm.a, compare_op=ALU.not_equal, fill=1.0,
                                                base=-127, pattern=[[1, 128]], channel_multiplier=1),
             r=[Jm.t], w=[Jm.t])
        S.op('pool', lambda e: e.memset(mask64.a, 1.0), w=[mask64.t])
        S.op('pool', lambda e: e.affine_select(out=mask64.a[0:64, :], in_=mask64.a[0:64, :], compare_op=ALU.is_ge,
                                                fill=0.0, base=0, pattern=[[1, 64]], channel_multiplier=-1),
             r=[mask64.t], w=[mask64.t])
        S.dma('sp', mask64.a[64:128, :], mask64.a[0:64, :], r=[mask64.t], w=[mask64.t])
        S.op('pool', lambda e: e.memset(selall.a, 0.0), w=[selall.t])
        S.op('pool', lambda e: e.affine_select(out=selall.a.rearrange("p (h m) -> p h m", m=128),
                                                in_=selall.a.rearrange("p (h m) -> p h m", m=128),
                                                compare_op=ALU.not_equal, fill=1.0, base=0,
                                                pattern=[[-1, 8], [0, 128]], channel_multiplier=1),
             r=[selall.t], w=[selall.t])
        S.op('pool', lambda e: e.memset(rm.a, 1.0), w=[rm.t])
        S.op('pool', lambda e: e.memset(rm.a.rearrange("p (c t) -> p c t", t=64)[:, :, 0:1], 0.0), r=[rm.t], w=[rm.t])
        S.op('pool', lambda e: e.memset(nm.a, 0.0), w=[nm.t])
        S.op('pool', lambda e: e.memset(nm.a.rearrange("p (c t) -> p c t", t=64)[:, :, 0:1], -1e30), r=[nm.t], w=[nm.t])
        S.op('pool', lambda e: e.memset(ones_bf.a, 1.0), w=[ones_bf.t])
        c64 = sb("c64", [128, 64], BF16)
        S.op('pool', lambda e: e.memset(c64.a, 1.0 / 64), w=[c64.t])
        eps6 = sb("eps6", [128, 1])
        S.op('pool', lambda e: e.memset(eps6.a, 1e-6), w=[eps6.t])

        def cast(dst, src, rows, nslab):
            toks = []
            step = rows // nslab
            for i in range(nslab):
                t = Tok()
                S.dma('pool', dst[i * step:(i + 1) * step, :], src[i * step:(i + 1) * step, :], w=[t])
                toks.append(t)
            return toks

        t_wbi = cast(wbi, w_in, D, 64)

        wtail = sb("wtail", [128, 16, 16], BF16)
        wconvT = sb("wconvT", [128, 4, 16])
        bconvT = sb("bconvT", [128, 16])
        gAT = sb("gAT", [128, 8])
        gBT = sb("gBT", [128, 8])
        big = sb("big", [8, 1])
        bfg = sb("bfg", [8, 1])
        nbf = sb("nbf", [8, 1])
        S.dma('sp', wtail.a, wbi[:, 7168:7184].rearrange("(kc p) n -> p kc n", p=128), r=t_wbi, w=[wtail.t])
        for j_ in range(4):
            S.dma('sp', wconvT.a[:, j_, :], w_conv[j_:j_ + 1, :].rearrange("o (fc p) -> p (o fc)", p=128), w=[wconvT.t],
                  allow_slow_non_contiguous=True)
        S.dma('sp', bconvT.a, b_conv.rearrange("o (fc p) -> p (o fc)", p=128), w=[bconvT.t], allow_slow_non_contiguous=True)
        S.dma('sp', gAT.a, g_attn.rearrange("o (c p) -> p (o c)", p=128), w=[gAT.t], allow_slow_non_contiguous=True)
        S.dma('sp', gBT.a, g_mlstm.rearrange("o (c p) -> p (o c)", p=128), w=[gBT.t], allow_slow_non_contiguous=True)
        S.dma('sp', big.a, b_ig, w=[big.t])
        S.dma('sp', bfg.a, b_fg, w=[bfg.t])
        S.op('dve', lambda e: e.tensor_scalar(nbf.a, bfg.a, -1.0, None, ALU.mult), r=[bfg.t], w=[nbf.t])

        kT = sb("kT", [128, 8, 1024], BF16)
        vA = sb("vA", [128, 8, 16, 64], BF16)
        hist = sb("hist", [128, 3, 16])
        Caug = sb("Caug", [128, 8, 129])
        Csbf = sb("Csbf", [128, 8, 129], BF16)
        mprev = sb("mprev", [8, 1])
        bc = sb("bc", [128, 8, 24])
        xT = sb("xT", [128, 16, 512], BF16)
        wbuf = [sb("wbuf%d" % i, [128, 16, 512], BF16) for i in range(2)]
        concatT = sb("concatT", [128, 16, 512], BF16)
        xs_st = sb("xs_st", [128, 2048])
        smalls = sb("smalls", [128, 128])
        smt_l = [Tok() for _ in range(4)]
        smt2 = [Tok() for _ in range(4)]
        gsm = sb("gsm", [8, 128])

        ARENA_B = 80 * 1024
        arena = S.sbuf("arena", [128, ARENA_B // 2], BF16)[:]

        class Carver:
            def __init__(self):
                self.off = 0

            def take(self, shape, dt):
                n = 1
                for s_ in shape[1:]:
                    n *= s_
                nbytes = n * (4 if dt == F32 else 2)
                nbytes = (nbytes + 63) // 64 * 64
                a = arena[:, self.off // 2:(self.off + nbytes) // 2]
                self.off += nbytes
                assert self.off <= ARENA_B, self.off
                if dt == F32:
                    a = a.bitcast(F32)
                a = a[:, 0:n]
                if len(shape) == 3:
                    a = a.rearrange("p (a b) -> p a b", b=shape[2])
                elif len(shape) == 4:
                    a = a.rearrange("p (a b c) -> p a b c", b=shape[2], c=shape[3])
                return Buf(a[0:shape[0]])

        cv_ = Carver()
        qT = cv_.take([128, 8, 512], BF16)
        tzb2 = [cv_.take([128, 640], F32) for _ in range(2)]
        tzb = tzb2[0]
        tzb4 = tzb2 + [cv_.take([128, 640], F32) for _ in range(2)]
        sbt = [cv_.take([128, 512], F32) for _ in range(3)]
        pT = [cv_.take([128, 512], BF16) for _ in range(6)]
        ocp = [cv_.take([128, 512], F32) for _ in range(2)]
        osq = [cv_.take([128, 512], BF16) for _ in range(2)]
        att_t = [cv_.take([128, 512], F32) for _ in range(2)]
        att_r = [cv_.take([128, 512], F32) for _ in range(2)]
        kst = att_t
        cv_ = Carver()
        qmT = cv_.take([128, 8, 512], BF16)
        kmT = cv_.take([128, 8, 512], BF16)
        gsT = cv_.take([128, 8, 512], F32)
        vB = cv_.take([128, 4, 8, 129], BF16)
        kTok = cv_.take([128, 4, 8, 128], BF16)
        raw = [cv_.take([128, 515], F32) for _ in range(2)]
        acc = cv_.take([128, 512], F32)
        sigt = cv_.take([128, 512], F32)
        G = [cv_.take([8, 512], F32) for _ in range(4)]
        eu_tok = cv_.take([128, 4, 8], F32)
        enf_tok = cv_.take([128, 4, 8], F32)
        Pm = [cv_.take([128, 64], BF16) for _ in range(6)]
        vpm = [cv_.take([128, 129], BF16) for _ in range(6)]
        updt = [cv_.take([128, 129], F32) for _ in range(6)]
        hn = [cv_.take([128, 128], F32) for _ in range(6)]
        sml = [cv_.take([128, 8], F32) for _ in range(6)]
        junk = cv_.take([128, 128], F32)
        cv_ = Carver()
        hT = cv_.take([128, 44, 512], BF16)
        lnA = cv_.take([128, 2048], F32)
        lnG = cv_.take([128, 2048], F32)
        lnB = cv_.take([128, 2048], F32)
        ostage = [cv_.take([128, 512], F32) for _ in range(2)]
        sg4 = [cv_.take([128, 512], F32) for _ in range(4)]

        t_wbo = cast(wbo, w_out, D, 32)
        t_wbg = cast(wbg, w_g, D, 64)
        t_wbu = cast(wbu, w_u, D, 64)
        t_wbd = cast(wbd, w_d, DFF, 88)

        t_tzd = Tok()
        rbs = tzb
        extS = sbt[1]
        S.dma('sp', rbs.a[0:16, 0:513], rel_bias, w=[rbs.t])
        S.op('dve', lambda e: e.tensor_copy(extS.a[0:16, 0:384], rbs.a[0:16, 129:513]), r=[rbs.t], w=[extS.t])
        S.op('dve', lambda e: e.tensor_scalar(att_t[0].a[0:16, 0:384], rbs.a[0:16, 0:384], 0.0, rbs.a[0:16, 512:513],
                                               ALU.mult, ALU.add), r=[rbs.t], w=[att_t[0].t])
        t_ext = Tok()
        S.dma('sp', extd[:, 0:384], extS.a[0:16, 0:384], r=[extS.t], w=[t_ext])
        S.dma('sp', extd[:, 384:768], att_t[0].a[0:16, 0:384], r=[att_t[0].t, t_ext], w=[t_ext])
        for h in range(16):
            src = bass.AP(tensor=extd.tensor, offset=h * 768, ap=[[1, 128], [1, 640]])
            S.dma('sp', tzb.a, src, r=[t_ext], w=[tzb.t])
            b0 = nb()
            S.op('pe', lambda e: e.matmul(bank[b0][:, 0:512], Jm.a, tzb.a[:, 0:512], start=True, stop=True),
                 r=[Jm.t, tzb.t], w=[PT[b0]])
            S.op('dve', lambda e: e.tensor_copy(ocp[h % 2].a[:, 0:512], bank[b0][:, 0:512]), r=[PT[b0]], w=[ocp[h % 2].t])
            b1 = nb()
            S.op('pe', lambda e: e.matmul(bank[b1][:, 0:128], Jm.a, tzb.a[:, 512:640], start=True, stop=True),
                 r=[Jm.t, tzb.t], w=[PT[b1]])
            S.op('act', lambda e: e.copy(att_t[h % 2].a[:, 0:128], bank[b1][:, 0:128]), r=[PT[b1]], w=[att_t[h % 2].t])
            S.dma('pool', tzd[h, :, 0:512], ocp[h % 2].a[:, 0:512], r=[ocp[h % 2].t], w=[t_tzd])
            S.dma('pool', tzd[h, :, 512:640], att_t[h % 2].a[:, 0:128], r=[att_t[h % 2].t], w=[t_tzd])
        S.barrier()

        evtog = [0]

        def evac_copy(out, in_, r, w):
            evtog[0] ^= 1
            if evtog[0]:
                S.op('act', lambda e: e.copy(out, in_), r=r, w=w)
            else:
                S.op('dve', lambda e: e.tensor_copy(out, in_), r=r, w=w)

        def accum(b, M, N, nk, lhs_fn, rhs_fn, r, start=True, stop=True):
            for kc in range(nk):
                S.op('pe', lambda e: e.matmul(bank[b][0:M, 0:N], lhs_fn(kc), rhs_fn(kc),
                                               start=(start and kc == 0), stop=(stop and kc == nk - 1)),
                     r=r, w=[PT[b]], inc=(kc == nk - 1))

        items = []

        def add(loads, run):
            items.append((loads, run))

        def wload(dram, k0, nk, c0, ncols, toks, dc0=0):
            return (dram, k0, nk, c0, ncols, toks, dc0)

        cflat = concatT.a.rearrange("p a b -> p (a b)")
        lnA2 = Buf(cflat[:, 0:4096].bitcast(F32))
        xs2 = Buf(cflat[:, 4096:8192].bitcast(F32))

        def ln_pass(T, src_scr, src_tok, res_src_fn, res_tok, g_d, b_d, dst_fn, dst_tok, make_xT):
            RP = min(128, T)
            NSB = (T + 127) // 128
            S.dma('sp', lnG.a, g_d.partition_broadcast(128), w=[lnG.t])
            S.dma('sp', lnB.a, b_d.partition_broadcast(128), w=[lnB.t])
            for sbi in range(NSB):
                r0 = sbi * 128
                if sbi % 2 == 0:
                    A, X, at, xt_ = lnA, xs_st, [lnA.t], [xs_st.t]
                else:
                    A, X, at, xt_ = lnA2, xs2, [lnA2.t, concatT.t], [xs2.t, concatT.t]
                st = smalls.a[0:RP, sbi * 32:sbi * 32 + 24]
                mv = smalls.a[0:RP, sbi * 32 + 24:sbi * 32 + 26]
                rstd = smalls.a[0:RP, sbi * 32 + 26:sbi * 32 + 27]
                nmr = smalls.a[0:RP, sbi * 32 + 27:sbi * 32 + 28]
                smt = smt_l[sbi]
                S.dma('sp', A.a[0:RP, :], src_scr[r0:r0 + RP, :], r=[src_tok], w=at)
                S.dma('sp', X.a[0:RP, :], res_src_fn(sbi), r=[res_tok], w=xt_)
                S.op('dve', lambda e: e.scalar_tensor_tensor(A.a[0:RP, :], X.a[0:RP, :], ALPHA, A.a[0:RP, :],
                                                              ALU.mult, ALU.add), r=xt_ + at, w=at)
                for q in range(4):
                    S.op('dve', lambda e: e.bn_stats(st[:, q * 6:(q + 1) * 6], A.a[0:RP, q * 512:(q + 1) * 512]),
                         r=at, w=[smt])
                S.op('dve', lambda e: e.bn_aggr(mv, st), r=[smt], w=[smt, smt2[sbi]])
                S.op('act', lambda e: e.activation(rstd, mv[:, 1:2], AF.Sqrt, bias=1e-5, scale=1.0), r=[smt], w=[smt])
                S.op('dve', lambda e: e.scalar_tensor_tensor(A.a[0:RP, :], A.a[0:RP, :], mv[:, 0:1], lnG.a[0:RP, :],
                                                              ALU.subtract, ALU.mult), r=[smt2[sbi], lnG.t] + at, w=at)
                S.op('dve', lambda e: e.reciprocal(rstd, rstd), r=[smt], w=[smt])
                S.op('dve', lambda e: e.scalar_tensor_tensor(A.a[0:RP, :], A.a[0:RP, :], rstd, lnB.a[0:RP, :],
                                                              ALU.mult, ALU.add), r=[smt, lnB.t] + at, w=at)
                S.dma('pool', dst_fn(sbi), A.a[0:RP, :], r=at, w=[dst_tok])
                if make_xT:
                    for q in range(4):
                        b = nb()
                        for j in range(4):
                            kc = q * 4 + j
                            S.op('pe', lambda e: e.transpose(bank[b][:, j * 128:j * 128 + RP],
                                                              A.a[0:RP, kc * 128:(kc + 1) * 128],
                                                              ident_f.a[0:RP, 0:RP]),
                                 r=at + [ident_f.t], w=[PT[b]], inc=(j == 3))
                        evac_copy(xT.a[:, q * 4:(q + 1) * 4, r0:r0 + RP],
                                  bank[b][:, 0:512].rearrange("p (j t) -> p j t", t=128)[:, :, 0:RP],
                                  r=[PT[b]], w=[xT.t])

        def emit_tile(kind, s, t):
            T = 512 if kind == 'p' else 64
            NCH = T // 64
            NSB = (T + 127) // 128
            RP = min(128, T)
            par = (t % 2) if kind == 'p' else 0
            first = (t == 0) if kind == 'p' else False
            last = (t == NT - 1) if kind == 'p' else True
            has_hist = (not first)
            if kind == 'p':
                row0 = s * SEQ + t * 512
                xsrc = lambda sbi: xp[row0 + sbi * 128: row0 + sbi * 128 + RP, :]
                ydst = lambda sbi: yp[row0 + sbi * 128: row0 + sbi * 128 + RP, :]
                k_out = lambda sbi, c0: kp[s * 512 + sbi * 128: s * 512 + sbi * 128 + RP, c0:c0 + 512]
                v_out = lambda sbi, c0: vp[s * 512 + sbi * 128: s * 512 + sbi * 128 + RP, c0:c0 + 512]
                conv_out = convp[s * 3:(s + 1) * 3, :]
                C_out = Cp[s * 8:(s + 1) * 8]
                n_out = np_[s * 8:(s + 1) * 8, :]
                m_out = mp[s * 8:(s + 1) * 8, :]
            else:
                xsrc = lambda sbi: xs[0:64, :]
                ydst = lambda sbi: ys[0:64, :]
                k_out = lambda sbi, c0: ks[448:512, c0:c0 + 512]
                v_out = lambda sbi, c0: vs[448:512, c0:c0 + 512]
                conv_out = convs
                C_out, n_out, m_out = Cs, ns, ms
            kcur = par * 512
            kprev = (1 - par) * 512

            def run_a0(_):
                if first:
                    S.op('pool', lambda e: e.memset(hist.a, 0.0), w=[hist.t])
                    S.op('pool', lambda e: e.memset(Caug.a, 0.0), w=[Caug.t])
                    S.op('pool', lambda e: e.memset(mprev.a, 0.0), w=[mprev.t])
                if kind == 's':
                    for j_ in range(3):
                        S.dma('sp', hist.a[:, j_, :], sconv[j_:j_ + 1, :].rearrange("o (fc p) -> p (o fc)", p=128),
                              w=[hist.t], allow_slow_non_contiguous=True)
                    S.dma('sp', Caug.a[:, :, 0:128], sC.rearrange("h d e -> d h e"), w=[Caug.t])
                    S.dma('sp', Caug.a[:, :, 128:129], sn.rearrange("h (d o) -> d h o", o=1), w=[Caug.t],
                          allow_slow_non_contiguous=True)
                    S.dma('sp', mprev.a, sm, w=[mprev.t])
                    S.dma('pool', ks[0:448, :], ck[64:512, :])
                    S.dma('pool', vs[0:448, :], cv[64:512, :])
                    for pb in range(4):
                        S.dma('sp', xs_st.a[:, 0:1024], cv[pb * 128:(pb + 1) * 128, :], w=[xs_st.t])
                        S.op('dve', lambda e: e.tensor_copy(vA.a[:, (1 - par) * 4 + pb, :, :],
                                                             xs_st.a[:, 0:1024].rearrange("p (h d) -> p h d", d=64)),
                             r=[xs_st.t], w=[vA.t])
                        S.dma('sp', xs_st.a[:, 1024:2048], ck[pb * 128:(pb + 1) * 128, :], w=[xs_st.t])
                        for q in range(2):
                            b = nb()
                            for j in range(4):
                                hp = q * 4 + j
                                S.op('pe', lambda e: e.transpose(bank[b][:, j * 128:(j + 1) * 128],
                                                                  xs_st.a[:, 1024 + hp * 128:1024 + (hp + 1) * 128],
                                                                  ident_f.a),
                                     r=[xs_st.t, ident_f.t], w=[PT[b]], inc=(j == 3))
                            evac_copy(kT.a[:, q * 4:(q + 1) * 4, kprev + pb * 128:kprev + (pb + 1) * 128],
                                      bank[b][:, 0:512].rearrange("p (j t) -> p j t", t=128), r=[PT[b]], w=[kT.t])
                for sbi in range(NSB):
                    for q in range(4):
                        xq = kst[q % 2]
                        S.dma('sp', xq.a[0:RP, :], xsrc(sbi)[:, q * 512:(q + 1) * 512], w=[xq.t])
                        b = nb()
                        for j in range(4):
                            S.op('pe', lambda e: e.transpose(bank[b][:, j * 128:j * 128 + RP],
                                                              xq.a[0:RP, j * 128:(j + 1) * 128],
                                                              ident_f.a[0:RP, 0:RP]),
                                 r=[xq.t, ident_f.t], w=[PT[b]], inc=(j == 3))
                        evac_copy(xT.a[:, q * 4:(q + 1) * 4, sbi * 128:sbi * 128 + RP],
                                  bank[b][:, 0:512].rearrange("p (j t) -> p j t", t=128)[:, :, 0:RP],
                                  r=[PT[b]], w=[xT.t])
            add(None, run_a0)

            def f_group(g, evac):
                def run(bi):
                    wb = wbuf[bi]
                    for cc in range(4):
                        b = nb()
                        accum(b, 128, T, 16, lambda kc: wb.a[:, kc, cc * 128:(cc + 1) * 128],
                              lambda kc: xT.a[:, kc, 0:T], r=[wb.t, xT.t])
                        evac(g * 4 + cc, b)
                add([wload(wbi, 0, 16, g * 512, 512, t_wbi)], run)

            def t_group(g, evac):
                def run(bi):
                    wb = wbuf[bi]
                    for sbi in range(NSB):
                        b = nb()
                        accum(b, RP, 512, 16, lambda kc: xT.a[:, kc, sbi * 128:sbi * 128 + RP],
                              lambda kc: wb.a[:, kc, 0:512], r=[wb.t, xT.t])
                        evac(g, sbi, b)
                add([wload(wbi, 0, 16, g * 512, 512, t_wbi)], run)

            def evac_qk(chunk, b):
                if chunk < 8:
                    evac_copy(qT.a[:, chunk, 0:T], bank[b][:, 0:T], r=[PT[b]], w=[qT.t])
                else:
                    evac_copy(kT.a[:, chunk - 8, kcur:kcur + T], bank[b][:, 0:T], r=[PT[b]], w=[kT.t])
            for g in (0, 1, 2, 3):
                f_group(g, evac_qk)

            ost = [0]

            def evac_v(g, sbi, b):
                blk = par * 4 + sbi
                S.op('dve', lambda e: e.tensor_copy(vA.a[0:RP, blk, (g - 4) * 8:(g - 4) * 8 + 8, :],
                                                     bank[b][0:RP, 0:512].rearrange("p (h d) -> p h d", d=64)),
                     r=[PT[b]], w=[vA.t])
                if last:
                    ost[0] ^= 1
                    o = kst[ost[0]]
                    S.op('act', lambda e: e.copy(o.a[0:RP, :], bank[b][0:RP, 0:512]), r=[PT[b]], w=[o.t])
                    S.dma('pool', v_out(sbi, (g - 4) * 512), o.a[0:RP, :], r=[o.t])
            for g in (4, 5):
                t_group(g, evac_v)

            if last:
                def evac_kf(g, sbi, b):
                    ost[0] ^= 1
                    o = kst[ost[0]]
                    S.op('act', lambda e: e.copy(o.a[0:RP, :], bank[b][0:RP, 0:512]), r=[PT[b]], w=[o.t])
                    S.dma('pool', k_out(sbi, (g - 2) * 512), o.a[0:RP, :], r=[o.t])
                for g in (2, 3):
                    t_group(g, evac_kf)

            def run_attn(_):
                SK = 3
                STB = (0, 1, 2)
                seq = []
                info = {}
                for h in range(16):
                    hp, po = h // 2, (h % 2) * 64
                    blocks = []
                    if has_hist:
                        for pb in range(4):
                            N = min(T, 128 * (pb + 1))
                            corner = (0, 64, N - 64, N) if 128 * (pb + 1) <= T else None
                            blocks.append((kT.a[po:po + 64, hp, kprev + pb * 128:kprev + (pb + 1) * 128],
                                           vA.a[:, (1 - par) * 4 + pb, h, :], 128, 0, N, 512 - 128 * pb, corner))
                    for cb in range(NSB):
                        nk = min(128, T - cb * 128)
                        q0 = cb * 128
                        N = T - q0
                        corner = (64, 128, 0, 64) if nk == 128 else None
                        blocks.append((kT.a[po:po + 64, hp, kcur + q0:kcur + q0 + nk],
                                       vA.a[0:nk, par * 4 + cb, h, :], nk, q0, N, 0, corner))
                    info[h] = blocks
                    for i in range(len(blocks)):
                        seq.append((h, i))

                def tz_load(h):
                    S.dma('sp', tzb4[h % 4].a, tzd[h], r=[t_tzd], w=[tzb4[h % 4].t])

                nblk = len(info[0])
                STBK = {0: 0, 1: 2}
                ABK = {0: 1, 1: 7}
                OSB = {0: (3, 4), 1: (5, 6)}
                pctr = [0]

                def e_st2(j):
                    p_, i = divmod(j, nblk)
                    if i == 0 and p_ + 1 < 8:
                        tz_load(2 * p_ + 2)
                        tz_load(2 * p_ + 3)
                    st = []
                    for par_ in (0, 1):
                        h = 2 * p_ + par_
                        hp, po = h // 2, par_ * 64
                        kap, vap, nk, q0, N, toff, corner = info[h][i]
                        b = STBK[par_]
                        S.op('pe', lambda e: e.matmul(bank[b][0:nk, 0:N], kap, qT.a[po:po + 64, hp, q0:q0 + N],
                                                       start=True, stop=True), r=[kT.t, qT.t], w=[PT[b]])
                    for par_ in (0, 1):
                        h = 2 * p_ + par_
                        kap, vap, nk, q0, N, toff, corner = info[h][i]
                        b = STBK[par_]
                        tmp = sbt[(2 * j + par_) % 3]
                        tz = tzb4[h % 4]
                        S.op('dve', lambda e: e.scalar_tensor_tensor(tmp.a[0:nk, 0:N], bank[b][0:nk, 0:N], 0.125,
                                                                      tz.a[0:nk, toff:toff + N], ALU.mult, ALU.add),
                             r=[PT[b], tz.t], w=[tmp.t])
                        P = pT[(2 * j + par_) % NP_]
                        S.op('act', lambda e: e.activation(P.a[0:nk, 0:N], tmp.a[0:nk, 0:N], AF.Exp),
                             r=[tmp.t], w=[P.t])
                        if corner is not None:
                            p0, p1, c0, c1 = corner
                            S.op('pool', lambda e: e.memset(P.a[p0:p1, c0:c1], 0.0), r=[P.t], w=[P.t])

                def e_pv2(j):
                    p_, i = divmod(j, nblk)
                    fst, lst = (i == 0), (i == nblk - 1)
                    for which in (0, 1):
                        for par_ in (0, 1):
                            h = 2 * p_ + par_
                            po = par_ * 64
                            kap, vap, nk, q0, N, toff, corner = info[h][i]
                            P = pT[(2 * j + par_) % NP_]
                            bk = OSB[par_][which]
                            lhs = vap[0:nk, :] if which == 0 else ones_bf.a[0:nk, :]
                            S.op('pe', lambda e: e.matmul(bank[bk][po:po + 64, q0:q0 + N], lhs, P.a[0:nk, 0:N],
                                                           start=fst, stop=lst, skip_group_check=True),
                                 r=[vA.t, ones_bf.t, P.t], w=[PT[bk]])
                    if lst:
                        for par_ in (0, 1):
                            tails.append(tail_stages(2 * p_ + par_))

                def tail_stages(h):
                    hp, po = h // 2, (h % 2) * 64
                    ob_, sb_ = OSB[h % 2]
                    ab = ABK[h % 2]
                    sl = slice(po, po + 64)
                    O, Q, rs, tt = ocp[h % 2], osq[h % 2], att_r[h % 2], att_t[h % 2]

                    def t1():
                        S.op('act', lambda e: e.activation(Q.a[sl, 0:T], bank[ob_][sl, 0:T], AF.Square), r=[PT[ob_]], w=[Q.t])
                        S.op('act', lambda e: e.copy(O.a[sl, 0:T], bank[ob_][sl, 0:T]), r=[PT[ob_]], w=[O.t])
                        S.op('act', lambda e: e.activation(rs.a[sl, 0:T], bank[sb_][sl, 0:T], AF.Square), r=[PT[sb_]], w=[rs.t])

                    def t2():
                        S.op('pe', lambda e: e.matmul(bank[ab][sl, 0:T], c64.a[sl, 0:64], Q.a[sl, 0:T], start=True, stop=True),
                             r=[c64.t, Q.t], w=[PT[ab]])

                    def t3():
                        S.op('dve', lambda e: e.scalar_tensor_tensor(tt.a[sl, 0:T], rs.a[sl, 0:T], 1e-6, bank[ab][sl, 0:T],
                                                                      ALU.mult, ALU.add), r=[PT[ab], rs.t], w=[tt.t])

                    def t4():
                        S.op('act', lambda e: e.activation(tt.a[sl, 0:T], tt.a[sl, 0:T], AF.Ln), r=[tt.t], w=[tt.t])
                        S.op('act', lambda e: e.activation(tt.a[sl, 0:T], tt.a[sl, 0:T], AF.Exp, scale=-0.5),
                             r=[tt.t], w=[tt.t])

                    def t5():
                        S.op('dve', lambda e: e.scalar_tensor_tensor(concatT.a[sl, hp, 0:T], O.a[sl, 0:T],
                                                                      gAT.a[sl, hp:hp + 1], tt.a[sl, 0:T], ALU.mult, ALU.mult),
                             r=[O.t, gAT.t, tt.t], w=[concatT.t])
                    return [t1, t2, t3, t4, t5]

                NP_ = 6
                SK2 = 2
                tails = []
                tz_load(0)
                tz_load(1)
                n2 = 8 * nblk
                for j in range(n2 + SK2):
                    if j < n2:
                        e_st2(j)
                    if j - SK2 >= 0:
                        e_pv2(j - SK2)
                    for tl in list(tails):
                        tl.pop(0)()
                        if not tl:
                            tails.remove(tl)
                while tails:
                    for tl in list(tails):
                        tl.pop(0)()
                        if not tl:
                            tails.remove(tl)
                S.barrier()
            add(None, run_attn)

            def evac_raw(chunk, b):
                if chunk < 48:
                    fc = chunk - 24
                    rb_ = raw[fc % 2]
                    S.op('dve', lambda e: e.tensor_copy(rb_.a[:, 0:3], hist.a[:, :, fc]), r=[hist.t], w=[rb_.t])
                    S.op('act', lambda e: e.copy(rb_.a[:, 3:3 + T], bank[b][:, 0:T]), r=[PT[b]], w=[rb_.t])
                    S.op('dve', lambda e: e.tensor_copy(hist.a[:, :, fc], rb_.a[:, T:T + 3]), r=[rb_.t], w=[hist.t])
                    S.op('dve', lambda e: e.tensor_scalar(acc.a[:, 0:T], rb_.a[:, 0:T], wconvT.a[:, 0, fc:fc + 1],
                                                           bconvT.a[:, fc:fc + 1], ALU.mult, ALU.add),
                         r=[rb_.t, wconvT.t, bconvT.t], w=[acc.t])
                    for j in (1, 2, 3):
                        S.op('dve', lambda e: e.scalar_tensor_tensor(acc.a[:, 0:T], rb_.a[:, j:j + T],
                                                                      wconvT.a[:, j, fc:fc + 1], acc.a[:, 0:T],
                                                                      ALU.mult, ALU.add), r=[rb_.t, acc.t], w=[acc.t])
                    S.op('act', lambda e: e.activation(sigt.a[:, 0:T], acc.a[:, 0:T], AF.Sigmoid), r=[acc.t], w=[sigt.t])
                    dst = qmT.a[:, fc, 0:T] if fc < 8 else kmT.a[:, fc - 8, 0:T]
                    dt_ = qmT.t if fc < 8 else kmT.t
                    sc = 1.0 if fc < 8 else KSCALE
                    S.op('dve', lambda e: e.scalar_tensor_tensor(dst, acc.a[:, 0:T], sc, sigt.a[:, 0:T], ALU.mult, ALU.mult),
                         r=[acc.t, sigt.t], w=[dt_])
                else:
                    hc = chunk - 48
                    S.op('act', lambda e: e.activation(gsT.a[:, hc, 0:T], bank[b][:, 0:T], AF.Sigmoid), r=[PT[b]], w=[gsT.t])
                    S.op('dve', lambda e: e.tensor_scalar(gsT.a[:, hc, 0:T], gsT.a[:, hc, 0:T], gBT.a[:, hc:hc + 1], None,
                                                           ALU.mult), r=[gsT.t, gBT.t], w=[gsT.t])
            for g in (6, 7, 8, 9, 12, 13):
                f_group(g, evac_raw)

            def evac_vb(g, sbi, b):
                S.op('act', lambda e: e.copy(vB.a[0:RP, sbi, (g - 10) * 4:(g - 10) * 4 + 4, 0:128],
                                             bank[b][0:RP, 0:512].rearrange("p (h d) -> p h d", d=128)),
                     r=[PT[b]], w=[vB.t])
            for g in (10, 11):
                t_group(g, evac_vb)

            def run_mlstm(_):
                S.op('pool', lambda e: e.memset(vB.a[:, :, :, 128:129], 1.0), r=[vB.t], w=[vB.t])
                if last:
                    for j_ in range(3):
                        S.dma('pool', conv_out[j_:j_ + 1, :].rearrange("o (fc p) -> p (o fc)", p=128), hist.a[:, j_, :],
                              r=[hist.t], allow_slow_non_contiguous=True)
                bi_, bf_ = nb(), nb()
                accum(bi_, 8, T, 16, lambda kc: wtail.a[:, kc, 0:8], lambda kc: xT.a[:, kc, 0:T], r=[wtail.t, xT.t])
                accum(bf_, 8, T, 16, lambda kc: wtail.a[:, kc, 8:16], lambda kc: xT.a[:, kc, 0:T], r=[wtail.t, xT.t])
                gi, ga, Fn, u = G[0], G[1], G[2], G[3]
                S.op('dve', lambda e: e.tensor_scalar(gi.a[:, 0:T], bank[bi_][0:8, 0:T], big.a[:, 0:1], None, ALU.add),
                     r=[PT[bi_], big.t], w=[gi.t])
                S.op('act', lambda e: e.activation(ga.a[:, 0:T], bank[bf_][0:8, 0:T], AF.Exp, bias=nbf.a[:, 0:1], scale=-1.0),
                     r=[PT[bf_], nbf.t], w=[ga.t])
                S.op('act', lambda e: e.activation(ga.a[:, 0:T], ga.a[:, 0:T], AF.Ln, bias=1.0, scale=1.0),
                     r=[ga.t], w=[ga.t])
                S.op('dve', lambda e: e.tensor_tensor_scan(Fn.a[:, 0:T], rm.a[:, 0:T], ga.a[:, 0:T], 0.0, ALU.mult, ALU.add),
                     r=[rm.t, ga.t], w=[Fn.t])
                S.op('dve', lambda e: e.tensor_tensor(u.a[:, 0:T], gi.a[:, 0:T], Fn.a[:, 0:T], ALU.add),
                     r=[gi.t, Fn.t], w=[u.t])
                cm = ga
                S.op('dve', lambda e: e.tensor_tensor_scan(cm.a[:, 0:T], nm.a[:, 0:T], u.a[:, 0:T], 0.0, ALU.add, ALU.max),
                     r=[nm.t, u.t], w=[cm.t])
                cmL = gsm.a[:, 0:NCH]
                FL = gsm.a[:, 8:8 + NCH]
                mall = gsm.a[:, 16:17 + NCH]
                ex = gsm.a[:, 32:32 + 3 * NCH]
                S.op('dve', lambda e: e.tensor_copy(cmL, cm.a[:, 0:T].rearrange("p (c t) -> p c t", t=64)[:, :, 63]),
                     r=[cm.t], w=[gsm.t])
                S.op('dve', lambda e: e.tensor_scalar(FL, Fn.a[:, 0:T].rearrange("p (c t) -> p c t", t=64)[:, :, 63],
                                                       -1.0, None, ALU.mult), r=[Fn.t], w=[gsm.t])
                S.op('dve', lambda e: e.tensor_copy(mall[:, 0:1], mprev.a), r=[mprev.t], w=[gsm.t])
                S.op('dve', lambda e: e.tensor_tensor_scan(mall[:, 1:NCH + 1], cmL, FL, mprev.a[:, 0:1], ALU.max, ALU.add),
                     r=[gsm.t, mprev.t], w=[gsm.t])
                S.op('dve', lambda e: e.tensor_copy(ex[:, 0:NCH], mall[:, 0:NCH]), r=[gsm.t], w=[gsm.t])
                S.op('dve', lambda e: e.tensor_tensor(ex[:, NCH:2 * NCH], FL, mall[:, 1:NCH + 1], ALU.subtract),
                     r=[gsm.t], w=[gsm.t])
                S.op('dve', lambda e: e.tensor_tensor(ex[:, 2 * NCH:3 * NCH], ex[:, NCH:2 * NCH], mall[:, 0:NCH], ALU.add),
                     r=[gsm.t], w=[gsm.t])
                S.op('dve', lambda e: e.tensor_copy(mprev.a, mall[:, NCH:NCH + 1]), r=[gsm.t], w=[mprev.t])
                if last:
                    S.dma('pool', m_out, mprev.a, r=[mprev.t])
                for sbi in range(NSB):
                    b = nb()
                    S.op('pe', lambda e: e.transpose(bank[b][0:RP, 0:8], u.a[:, sbi * 128:sbi * 128 + RP], ident_f.a[0:8, 0:8]),
                         r=[u.t, ident_f.t], w=[PT[b]])
                    S.op('act', lambda e: e.activation(eu_tok.a[0:RP, sbi, :], bank[b][0:RP, 0:8], AF.Exp),
                         r=[PT[b]], w=[eu_tok.t])
                    b = nb()
                    S.op('pe', lambda e: e.transpose(bank[b][0:RP, 0:8], Fn.a[:, sbi * 128:sbi * 128 + RP], ident_f.a[0:8, 0:8]),
                         r=[Fn.t, ident_f.t], w=[PT[b]])
                    S.op('act', lambda e: e.activation(enf_tok.a[0:RP, sbi, :], bank[b][0:RP, 0:8], AF.Exp),
                         r=[PT[b]], w=[enf_tok.t])
                CaT = [Tok() for _ in range(8)]
                CsT = [Tok() for _ in range(8)]
                for h in range(8):
                    b = nb()
                    S.op('pe', lambda e: e.matmul(bank[b][:, 0:3 * NCH], selall.a[:, h * 128:(h + 1) * 128], ex,
                                                   start=True, stop=True), r=[selall.t, gsm.t], w=[PT[b]])
                    S.op('act', lambda e: e.activation(bc.a[:, h, 0:3 * NCH], bank[b][:, 0:3 * NCH], AF.Exp),
                         r=[PT[b]], w=[bc.t])
                    S.op('act', lambda e: e.activation(Csbf.a[:, h, :], Caug.a[:, h, :], AF.Copy, scale=bc.a[:, h, 0:1]),
                         r=[Caug.t, bc.t], w=[Csbf.t, CsT[h]])
                for sbi in range(NSB):
                    b = nb()
                    bb = bank[b].bitcast(BF16)
                    for h in range(8):
                        S.op('pe', lambda e: e.transpose(bb[0:RP, h * 128:(h + 1) * 128],
                                                          kmT.a[:, h, sbi * 128:sbi * 128 + RP], ident_b.a),
                             r=[kmT.t, ident_b.t], w=[PT[b]], inc=(h == 7))
                    evac_copy(kTok.a[0:RP, sbi, :, :], bb[0:RP, :].rearrange("p (h d) -> p h d", d=128), r=[PT[b]], w=[kTok.t])
                steps = [(c, h) for c in range(NCH) for h in range(8)]
                NB_ = 6

                def geo(i):
                    c, h = steps[i]
                    sbi, pbase = c // 2, (c % 2) * 64
                    return c, h, sbi, pbase, slice(pbase, pbase + 64), slice(c * 64, c * 64 + 64)

                def s0(i):
                    c, h, sbi, pbase, sl, cols = geo(i)
                    b1 = (0, 1)[i % 2]
                    S.op('pe', lambda e: e.matmul(bank[b1][sl, 0:64], kmT.a[:, h, cols], qmT.a[:, h, cols],
                                                   start=True, stop=True), r=[kmT.t, qmT.t], w=[PT[b1]])

                def s1(i):
                    c, h, sbi, pbase, sl, cols = geo(i)
                    b1 = (0, 1)[i % 2]
                    P, vv = Pm[i % NB_], vpm[i % NB_]
                    S.op('dve', lambda e: e.tensor_tensor(P.a[sl, :], bank[b1][sl, 0:64], mask64.a[sl, :], ALU.mult),
                         r=[PT[b1], mask64.t], w=[P.t])
                    S.op('pool' if i % 2 else 'dve',
                         lambda e: e.tensor_scalar(vv.a[sl, :], vB.a[sl, sbi, h, :], eu_tok.a[sl, sbi, h:h + 1],
                                                   None, ALU.mult), r=[vB.t, eu_tok.t], w=[vv.t])

                def s2(i):
                    c, h, sbi, pbase, sl, cols = geo(i)
                    P, vv = Pm[i % NB_], vpm[i % NB_]
                    b2 = (2, 3)[i % 2]
                    b3 = (4, 5)[i % 2]
                    S.op('pe', lambda e: e.matmul(bank[b2][sl, 0:129], P.a[sl, :], vv.a[sl, :], start=True, stop=False),
                         r=[P.t, vv.t], w=[PT[b2]], inc=False)
                    S.op('pe', lambda e: e.matmul(bank[b2][sl, 0:129], qmT.a[:, h, cols], Csbf.a[:, h, :],
                                                   start=False, stop=True), r=[qmT.t, CsT[h]], w=[PT[b2]])
                    S.op('pe', lambda e: e.matmul(bank[b3][:, 0:129], kTok.a[sl, sbi, h, :], vv.a[sl, :],
                                                   start=True, stop=True), r=[kTok.t, vv.t], w=[PT[b3]])

                def s3(i):
                    c, h, sbi, pbase, sl, cols = geo(i)
                    b2 = (2, 3)[i % 2]
                    b3 = (4, 5)[i % 2]
                    ut, hh, sm_ = updt[i % NB_], hn[i % NB_], sml[i % NB_]
                    S.op('act', lambda e: e.activation(ut.a, bank[b3][:, 0:129], AF.Copy,
                                                       scale=bc.a[:, h, NCH + c:NCH + c + 1]),
                         r=[PT[b3], bc.t], w=[ut.t])
                    dd, rec = sm_.a[sl, 0:1], sm_.a[sl, 1:2]
                    S.op('dve', lambda e: e.tensor_scalar(rec, bank[b2][sl, 128:129], -1.0, enf_tok.a[sl, sbi, h:h + 1],
                                                           ALU.mult, ALU.max), r=[PT[b2], enf_tok.t], w=[sm_.t])
                    S.op('dve', lambda e: e.tensor_tensor(dd, rec, bank[b2][sl, 128:129], ALU.max),
                         r=[PT[b2], sm_.t], w=[sm_.t])
                    S.op('dve', lambda e: e.reciprocal(rec, dd), r=[sm_.t], w=[sm_.t])
                    S.op('dve', lambda e: e.tensor_scalar(hh.a[sl, :], bank[b2][sl, 0:128], rec, None, ALU.mult),
                         r=[PT[b2], sm_.t], w=[hh.t])

                def s4(i):
                    c, h, sbi, pbase, sl, cols = geo(i)
                    ut, hh, sm_ = updt[i % NB_], hn[i % NB_], sml[i % NB_]
                    ss = sm_.a[sl, 2:3]
                    S.op('dve', lambda e: e.scalar_tensor_tensor(Caug.a[:, h, :], Caug.a[:, h, :],
                                                                  bc.a[:, h, 2 * NCH + c:2 * NCH + c + 1], ut.a,
                                                                  ALU.mult, ALU.add), r=[CaT[h], bc.t, ut.t], w=[CaT[h]])
                    if c + 1 < NCH:
                        S.op('act', lambda e: e.activation(Csbf.a[:, h, :], Caug.a[:, h, :], AF.Copy,
                                                           scale=bc.a[:, h, c + 1:c + 2]),
                             r=[CaT[h], bc.t], w=[CsT[h]])
                    S.op('act', lambda e: e.activation(junk.a[sl, 0:128], hh.a[sl, :], AF.Square, accum_out=ss),
                         r=[hh.t], w=[junk.t, sm_.t])
                    S.op('act', lambda e: e.activation(ss, ss, AF.Ln, bias=eps6.a[sl, 0:1], scale=1.0 / 128),
                         r=[sm_.t, eps6.t], w=[sm_.t])
                    S.op('act', lambda e: e.activation(ss, ss, AF.Exp, scale=-0.5), r=[sm_.t], w=[sm_.t])
                    S.op('act', lambda e: e.activation(hh.a[sl, :], hh.a[sl, :], AF.Copy, scale=ss),
                         r=[hh.t, sm_.t], w=[hh.t])

                def s6(i):
                    c, h, sbi, pbase, sl, cols = geo(i)
                    hh = hn[i % NB_]
                    b4 = (6, 7)[i % 2]
                    S.op('pe', lambda e: e.transpose(bank[b4][:, 0:64], hh.a[sl, :], ident_f.a[sl, pbase:pbase + 64]),
                         r=[hh.t, ident_f.t], w=[PT[b4]])

                def s7(i):
                    c, h, sbi, pbase, sl, cols = geo(i)
                    b4 = (6, 7)[i % 2]
                    S.op('dve', lambda e: e.tensor_tensor(concatT.a[:, 8 + h, cols], bank[b4][:, 0:64], gsT.a[:, h, cols],
                                                           ALU.mult), r=[PT[b4], gsT.t], w=[concatT.t])

                stages = [s0, s1, s2, s3, s4, s6, s7]
                ns_ = len(steps)
                for it_ in range(ns_ + len(stages) - 1):
                    for k_, fn in enumerate(stages):
                        i = it_ - k_
                        if 0 <= i < ns_:
                            fn(i)
                S.op('dve', lambda e: e.tensor_copy(sml[0].a[0:8, 0:1], sml[0].a[0:8, 1:2]), r=CaT + CsT, w=[Caug.t, Csbf.t])
                if last:
                    S.dma('pool', C_out.rearrange("h d e -> d h e"), Caug.a[:, :, 0:128], r=[Caug.t])
                    S.dma('pool', n_out.rearrange("h (d o) -> d h o", o=1), Caug.a[:, :, 128:129], r=[Caug.t],
                          allow_slow_non_contiguous=True)
                S.barrier()
            add(None, run_mlstm)

            for cg in range(4):
                def run(bi, cg=cg):
                    wb = wbuf[bi]
                    for sbi in range(NSB):
                        b = nb()
                        accum(b, RP, 512, 16, lambda kc: concatT.a[:, kc, sbi * 128:sbi * 128 + RP],
                              lambda kc: wb.a[:, kc, 0:512], r=[wb.t, concatT.t])
                        ost[0] ^= 1
                        o = ostage[ost[0]]
                        evac_copy(o.a[0:RP, :], bank[b][0:RP, 0:512], r=[PT[b]], w=[o.t])
                        S.dma('pool', mixd[sbi * 128:sbi * 128 + RP, cg * 512:(cg + 1) * 512], o.a[0:RP, :], r=[o.t], w=[t_mixd])
                add([wload(wbo, 0, 16, cg * 512, 512, t_wbo)], run)

            def run_ln1(_):
                ln_pass(T, mixd, t_mixd, xsrc, Tok(), ln1_g, ln1_b, lambda sbi: x1d[sbi * 128:sbi * 128 + RP, :], t_x1d, True)
            add(None, run_ln1)

            for k in range(11):
                def run_g(bi, k=k):
                    wb = wbuf[bi]
                    for cc in range(4):
                        bA = nb()
                        accum(bA, 128, T, 16, lambda kc: wb.a[:, kc, cc * 128:(cc + 1) * 128],
                              lambda kc: xT.a[:, kc, 0:T], r=[wb.t, xT.t])
                        S.op('act', lambda e: e.activation(sg4[cc].a[:, 0:T], bank[bA][:, 0:T], AF.Silu),
                             r=[PT[bA]], w=[sg4[cc].t])
                add([wload(wbg, 0, 16, k * 512, 512, t_wbg)], run_g)

                def run_u(bi, k=k):
                    wb = wbuf[bi]
                    for cc in range(4):
                        bB = nb()
                        accum(bB, 128, T, 16, lambda kc: wb.a[:, kc, cc * 128:(cc + 1) * 128],
                              lambda kc: xT.a[:, kc, 0:T], r=[wb.t, xT.t])
                        S.op('dve', lambda e: e.tensor_tensor(hT.a[:, k * 4 + cc, 0:T], sg4[cc].a[:, 0:T], bank[bB][:, 0:T],
                                                               ALU.mult), r=[sg4[cc].t, PT[bB]], w=[hT.t])
                add([wload(wbu, 0, 16, k * 512, 512, t_wbu)], run_u)

            for cg in range(4):
                for piece, (k0, nk) in enumerate(((0, 16), (16, 16), (32, 12))):
                    def run(bi, cg=cg, piece=piece, k0=k0, nk=nk):
                        wb = wbuf[bi]
                        for sbi in range(NSB):
                            b = sbi
                            accum(b, RP, 512, nk, lambda kc: hT.a[:, k0 + kc, sbi * 128:sbi * 128 + RP],
                                  lambda kc: wb.a[:, kc, 0:512], r=[wb.t, hT.t], start=(piece == 0), stop=(piece == 2))
                            if piece == 2:
                                ost[0] ^= 1
                                o = ostage[ost[0]]
                                evac_copy(o.a[0:RP, :], bank[b][0:RP, 0:512], r=[PT[b]], w=[o.t])
                                S.dma('pool', ypd[sbi * 128:sbi * 128 + RP, cg * 512:(cg + 1) * 512], o.a[0:RP, :],
                                      r=[o.t], w=[t_ypd])
                    add([wload(wbd, k0, nk, cg * 512, 512, t_wbd)], run)

            def run_ln2(_):
                ln_pass(T, ypd, t_ypd, lambda sbi: x1d[sbi * 128:sbi * 128 + RP, :], t_x1d, ln2_g, ln2_b, ydst, Tok(), False)
                S.barrier_pe()
            add(None, run_ln2)

        t_mixd, t_ypd, t_x1d = Tok(), Tok(), Tok()
        for s in range(NSEQ):
            for t in range(NT):
                emit_tile('p', s, t)
        if SAMPLE:
            emit_tile('s', 0, 0)

        widx = [i for i, it in enumerate(items) if it[0] is not None]
        ptr = [0]
        bufof = {}

        def issue_next():
            if ptr[0] < len(widx):
                i = widx[ptr[0]]
                bi = ptr[0] % 2
                bufof[i] = bi
                for (dram, k0, nk, c0, ncols, toks, dc0) in items[i][0]:
                    S.dma('sp', wbuf[bi].a[:, 0:nk, dc0:dc0 + ncols],
                          dram[k0 * 128:(k0 + nk) * 128, c0:c0 + ncols].rearrange("(kc p) n -> p kc n", p=128),
                          r=toks, w=[wbuf[bi].t])
                ptr[0] += 1
        issue_next()
        for i, (loads, run) in enumerate(items):
            if loads is not None:
                issue_next()
                run(bufof[i])
            else:
                run(None)
        S.finish()
    return nc


_NC_CACHE = {}


def _prep_common(inp):
    f = lambda a: np.ascontiguousarray(a, dtype=np.float32)
    return {
        "w_in": f(inp["w_in"][0]), "b_ig": f(inp["b_igate"][0].reshape(8, 1)), "b_fg": f(inp["b_fgate"][0].reshape(8, 1)),
        "w_conv": f(inp["w_conv"][0]), "b_conv": f(inp["b_conv"][0].reshape(1, 2048)),
        "rel_bias": f(inp["rel_bias"][0]), "g_attn": f(inp["g_attn_norm"][0].reshape(1, 1024)),
        "g_mlstm": f(inp["g_mlstm_norm"][0].reshape(1, 1024)), "w_out": f(inp["w_out"][0]),
        "ln1_g": f(inp["ln1_g"][0].reshape(1, D)), "ln1_b": f(inp["ln1_b"][0].reshape(1, D)),
        "w_g": f(inp["w_ffn_gate"][0]), "w_u": f(inp["w_ffn_up"][0]), "w_d": f(inp["w_ffn_down"][0]),
        "ln2_g": f(inp["ln2_g"][0].reshape(1, D)), "ln2_b": f(inp["ln2_b"][0].reshape(1, D)),
    }


def kernel(**inp):
    n = 8
    if "nc" not in _NC_CACHE:
        _NC_CACHE["nc"] = build()
    nc = _NC_CACHE["nc"]
    common = _prep_common(inp)
    f = lambda a: np.ascontiguousarray(a, dtype=np.float32)
    in_maps = []
    for c in range(n):
        m = dict(common)
        m["xp"] = f(inp["x_prompt"][2 * c:2 * c + 2].reshape(2 * 2048, D))
        m["xs"] = f(inp["x_sample"][c])
        m["ck"] = f(inp["cache_attn_k"][0, c].reshape(512, 1024))
        m["cv"] = f(inp["cache_attn_v"][0, c].reshape(512, 1024))
        m["sconv"] = f(inp["state_conv"][0, c])
        m["sC"] = f(inp["state_mlstm_C"][0, c])
        m["sn"] = f(inp["state_mlstm_n"][0, c])
        m["sm"] = f(inp["state_mlstm_m"][0, c].reshape(8, 1))
        in_maps.append(m)
    res = run_bass_kernel_spmd(nc, in_maps, core_ids=list(range(n)))
    R = res.results
    cat = lambda k: np.concatenate([r[k] for r in R], axis=0)
    y_p = cat("yp").reshape(16, 2048, D)
    y_s = np.stack([r["ys"] for r in R]).reshape(8, 64, D)
    k_p = cat("kp").reshape(1, 16, 512, 16, 64)
    v_p = cat("vp").reshape(1, 16, 512, 16, 64)
    conv_p = cat("convp").reshape(1, 16, 3, 2048)
    C_p = cat("Cp").reshape(1, 16, 8, 128, 128)
    n_p = cat("np").reshape(1, 16, 8, 128)
    m_p = cat("mp").reshape(1, 16, 8)
    k_s = np.stack([r["ks"] for r in R]).reshape(1, 8, 512, 16, 64)
    v_s = np.stack([r["vs"] for r in R]).reshape(1, 8, 512, 16, 64)
    conv_s = np.stack([r["convs"] for r in R]).reshape(1, 8, 3, 2048)
    C_s = np.stack([r["Cs"] for r in R]).reshape(1, 8, 8, 128, 128)
    n_s = np.stack([r["ns"] for r in R]).reshape(1, 8, 8, 128)
    m_s = np.stack([r["ms"] for r in R]).reshape(1, 8, 8)
    outs = (y_p, y_s, k_p, v_p, conv_p, C_p, n_p, m_p, k_s, v_s, conv_s, C_s, n_s, m_s)
    return tuple(np.ascontiguousarray(o, dtype=np.float32) for o in outs)
```
